# Optimizing a Trainium2 kernel written in Bass

```python
import math
import jax, jax.numpy as jnp
from jax import lax
import numpy as np

D_MODEL = 1024
BATCH = 16
SEQ = 4096
DEPTH = 1

HEAD_DIM = 64
N_HEADS_FOX = 8
N_HEADS_DIL = 8
WIDTH_FOX = N_HEADS_FOX * HEAD_DIM
WIDTH_DIL = N_HEADS_DIL * HEAD_DIM
D_MIX = WIDTH_FOX + WIDTH_DIL
D_IN = 3 * WIDTH_FOX + N_HEADS_FOX + 3 * WIDTH_DIL
BLOCK = 128
DILATED_PATTERNS = ((128, 1), (512, 4), (2048, 16))
ROPE_THETA = 500000.0
ROPE_DIMS = HEAD_DIM // 4
D_FF = 2816
CONV_WIDTH = 3
DEEPNORM_ALPHA = (2.0 * DEPTH) ** 0.25
DEEPNORM_BETA = (8.0 * DEPTH) ** -0.25
LN_EPS = 1e-5
RMS_EPS = 1e-6

kernel_name = "fox_dilated_hybrid_deepnorm_block"


def _layer_norm(x, g, b):
    xf = x.astype(jnp.float32)
    mu = jnp.mean(xf, axis=-1, keepdims=True)
    var = jnp.mean(jnp.square(xf - mu), axis=-1, keepdims=True)
    y = (xf - mu) * lax.rsqrt(var + LN_EPS)
    return (y * g.astype(jnp.float32) + b.astype(jnp.float32)).astype(x.dtype)


def _head_rms_norm(o, gain):
    of = o.astype(jnp.float32)
    of = of * lax.rsqrt(jnp.mean(jnp.square(of), axis=-1, keepdims=True) + RMS_EPS)
    B, S, H, Dh = o.shape
    return (of.reshape(B, S, H * Dh) * gain.astype(jnp.float32)).astype(o.dtype)


def _partial_rotary(t, positions):
    half = ROPE_DIMS // 2
    freqs = ROPE_THETA ** (-jnp.arange(0, ROPE_DIMS, 2, dtype=jnp.float32) / ROPE_DIMS)
    ang = positions.astype(jnp.float32)[:, :, None] * freqs
    cos = jnp.cos(ang)[:, :, None, :]
    sin = jnp.sin(ang)[:, :, None, :]
    tf = t.astype(jnp.float32)
    t1, t2, rest = tf[..., :half], tf[..., half:ROPE_DIMS], tf[..., ROPE_DIMS:]
    rot = jnp.concatenate([t1 * cos - t2 * sin, t2 * cos + t1 * sin, rest], axis=-1)
    return rot.astype(t.dtype)


def _forgetting_attention(q, k, v, log_f):
    B, S, H, Dh = q.shape
    nb = S // BLOCK
    scale = Dh ** -0.5
    F = jnp.cumsum(log_f, axis=1).transpose(0, 2, 1)
    qh, kh, vh = (t.transpose(0, 2, 1, 3) for t in (q, k, v))
    q_blocks = jnp.moveaxis(qh.reshape(B, H, nb, BLOCK, Dh), 2, 0)
    F_blocks = jnp.moveaxis(F.reshape(B, H, nb, BLOCK), 2, 0)
    k_pos = jnp.arange(S)

    def one_block(args):
        qb, Fq, n = args
        s = jnp.einsum('bhqd,bhkd->bhqk', qb, kh).astype(jnp.float32) * scale
        s = s + Fq[..., :, None] - F[:, :, None, :]
        q_pos = n * BLOCK + jnp.arange(BLOCK)
        s = jnp.where(k_pos[None, :] <= q_pos[:, None], s, -jnp.inf)
        p = jax.nn.softmax(s, axis=-1)
        return jnp.einsum('bhqk,bhkd->bhqd', p.astype(vh.dtype), vh)

    o = lax.map(one_block, (q_blocks, F_blocks, jnp.arange(nb)))
    o = jnp.moveaxis(o, 0, 2).reshape(B, H, S, Dh)
    return o.transpose(0, 2, 1, 3)


def _dilated_pattern(q, k, v, dilation, steps):
    B, S, H, Dh = q.shape
    L = S // dilation
    BB = B * dilation
    scale = Dh ** -0.5

    def to_sub(t):
        return t.reshape(B, L, dilation, H, Dh).transpose(0, 2, 3, 1, 4).reshape(BB, H, L, Dh)

    nb = -(-L // BLOCK)
    Lp = nb * BLOCK
    pad = ((0, 0), (0, 0), (0, Lp - L), (0, 0))
    qs, ks, vs = (jnp.pad(to_sub(t), pad).reshape(BB, H, nb, BLOCK, Dh) for t in (q, k, v))
    blk_pad = ((0, 0), (0, 0), (1, 0), (0, 0), (0, 0))
    k_cat = jnp.concatenate([jnp.pad(ks, blk_pad)[:, :, :-1], ks], axis=3)
    v_cat = jnp.concatenate([jnp.pad(vs, blk_pad)[:, :, :-1], vs], axis=3)

    s = jnp.einsum('bhnqd,bhnkd->bhnqk', qs, k_cat).astype(jnp.float32) * scale
    qi = jnp.arange(BLOCK)[:, None]
    kj = jnp.arange(2 * BLOCK)[None, :]
    dist = qi + BLOCK - kj
    k_pos = jnp.arange(nb)[:, None, None] * BLOCK + kj - BLOCK
    mask = (dist >= 0) & (dist <= steps) & (k_pos >= 0)
    s = jnp.where(mask, s, -jnp.inf)
    m = jnp.max(s, axis=-1, keepdims=True)
    p = jnp.exp(s - m)
    l = jnp.sum(p, axis=-1, keepdims=True)
    o = jnp.einsum('bhnqk,bhnkd->bhnqd', (p / l).astype(v.dtype), v_cat)
    lse = (m + jnp.log(l))[..., 0]

    o = o.reshape(BB, H, Lp, Dh)[:, :, :L]
    o = o.reshape(B, dilation, H, L, Dh).transpose(0, 3, 1, 2, 4).reshape(B, S, H, Dh)
    lse = lse.reshape(BB, H, Lp)[:, :, :L]
    lse = lse.reshape(B, dilation, H, L).transpose(0, 3, 1, 2).reshape(B, S, H)
    return o, lse


def _dilated_attention(q, k, v):
    outs, lses = [], []
    for window, dilation in DILATED_PATTERNS:
        o, lse = _dilated_pattern(q, k, v, dilation, window // dilation)
        outs.append(o)
        lses.append(lse)
    w = jax.nn.softmax(jnp.stack(lses, axis=0), axis=0)
    o = jnp.sum(w[..., None] * jnp.stack(outs, axis=0).astype(jnp.float32), axis=0)
    return o.astype(q.dtype)


def _token_mixer(h, positions, w_in, b_fgate, gn_a, gn_b, w_out):
    B, S, _ = h.shape
    z = h @ w_in
    o0 = 0
    qa = z[..., o0:o0 + WIDTH_FOX]; o0 += WIDTH_FOX
    ka = z[..., o0:o0 + WIDTH_FOX]; o0 += WIDTH_FOX
    va = z[..., o0:o0 + WIDTH_FOX]; o0 += WIDTH_FOX
    fa = z[..., o0:o0 + N_HEADS_FOX]; o0 += N_HEADS_FOX
    qb = z[..., o0:o0 + WIDTH_DIL]; o0 += WIDTH_DIL
    kb = z[..., o0:o0 + WIDTH_DIL]; o0 += WIDTH_DIL
    vb = z[..., o0:o0 + WIDTH_DIL]

    heads = lambda t, H: t.reshape(B, S, H, HEAD_DIM)
    log_f = jax.nn.log_sigmoid((fa + b_fgate).astype(jnp.float32))
    oa = _forgetting_attention(heads(qa, N_HEADS_FOX), heads(ka, N_HEADS_FOX),
                               heads(va, N_HEADS_FOX), log_f)
    qb = _partial_rotary(heads(qb, N_HEADS_DIL), positions)
    kb = _partial_rotary(heads(kb, N_HEADS_DIL), positions)
    ob = _dilated_attention(qb, kb, heads(vb, N_HEADS_DIL))

    merged = jnp.concatenate([_head_rms_norm(oa, gn_a), _head_rms_norm(ob, gn_b)], axis=-1)
    return merged @ w_out


def _conv_ffn(h, w_up, conv_w, conv_b, w_down):
    u = h @ w_up
    up = jnp.pad(u, ((0, 0), (CONV_WIDTH - 1, 0), (0, 0)))
    S = h.shape[1]
    y = conv_b + sum(up[:, i:i + S] * conv_w[i] for i in range(CONV_WIDTH))
    a, g = jnp.split(y, 2, axis=-1)
    return (jax.nn.silu(g) * a) @ w_down


def setup_inputs(seed: int = 0) -> dict:
    key = jax.random.key(seed)
    ks = jax.random.split(key, 20)
    n = jax.random.normal
    f32 = jnp.float32
    x = n(ks[0], (BATCH, SEQ, D_MODEL), f32)
    c = n(ks[1], (BATCH, D_MODEL), f32)
    offset = jax.random.randint(ks[2], (BATCH, 1), 0, 1024, dtype=jnp.int32)
    positions = (offset + jnp.arange(SEQ, dtype=jnp.int32)[None, :]).astype(jnp.int32)
    w_ada = n(ks[3], (DEPTH, D_MODEL, 6 * D_MODEL), f32) * D_MODEL ** -0.5
    b_ada = 0.02 * n(ks[4], (DEPTH, 6 * D_MODEL), f32)
    w_in = n(ks[5], (DEPTH, D_MODEL, D_IN), f32) * D_MODEL ** -0.5
    b_fgate = jnp.linspace(1.0, 6.0, N_HEADS_FOX, dtype=f32)[None, :] + 0.1 * n(ks[6], (DEPTH, N_HEADS_FOX), f32)
    gn_a = 1.0 + 0.02 * n(ks[7], (DEPTH, WIDTH_FOX), f32)
    gn_b = 1.0 + 0.02 * n(ks[8], (DEPTH, WIDTH_DIL), f32)
    w_out = n(ks[9], (DEPTH, D_MIX, D_MODEL), f32) * D_MIX ** -0.5 * DEEPNORM_BETA
    ln1_g = 1.0 + 0.02 * n(ks[10], (DEPTH, D_MODEL), f32)
    ln1_b = 0.02 * n(ks[11], (DEPTH, D_MODEL), f32)
    w_up = n(ks[12], (DEPTH, D_MODEL, 2 * D_FF), f32) * D_MODEL ** -0.5
    conv_w = n(ks[13], (DEPTH, CONV_WIDTH, 2 * D_FF), f32) * CONV_WIDTH ** -0.5
    conv_b = 0.02 * n(ks[14], (DEPTH, 2 * D_FF), f32)
    w_down = n(ks[15], (DEPTH, D_FF, D_MODEL), f32) * D_FF ** -0.5 * DEEPNORM_BETA
    ln2_g = 1.0 + 0.02 * n(ks[16], (DEPTH, D_MODEL), f32)
    ln2_b = 0.02 * n(ks[17], (DEPTH, D_MODEL), f32)
    return {"x": x, "c": c, "positions": positions, "w_ada": w_ada, "b_ada": b_ada,
            "w_in": w_in, "b_fgate": b_fgate, "gn_a": gn_a, "gn_b": gn_b, "w_out": w_out,
            "ln1_g": ln1_g, "ln1_b": ln1_b, "w_up": w_up, "conv_w": conv_w, "conv_b": conv_b,
            "w_down": w_down, "ln2_g": ln2_g, "ln2_b": ln2_b}


def reference(x, c, positions, w_ada, b_ada, w_in, b_fgate, gn_a, gn_b, w_out,
              ln1_g, ln1_b, w_up, conv_w, conv_b, w_down, ln2_g, ln2_b):
    for l in range(DEPTH):
        ada = jax.nn.silu(c) @ w_ada[l] + b_ada[l]
        sh_a, sc_a, g_a, sh_f, sc_f, g_f = (t[:, None, :] for t in jnp.split(ada, 6, axis=-1))
        h = x * (1.0 + sc_a) + sh_a
        mix = _token_mixer(h, positions, w_in[l], b_fgate[l], gn_a[l], gn_b[l], w_out[l])
        x = _layer_norm(DEEPNORM_ALPHA * x + g_a * mix, ln1_g[l], ln1_b[l])
        h = x * (1.0 + sc_f) + sh_f
        ffn = _conv_ffn(h, w_up[l], conv_w[l], conv_b[l], w_down[l])
        x = _layer_norm(DEEPNORM_ALPHA * x + g_f * ffn, ln2_g[l], ln2_b[l])
    return x
```

```python
import math
from contextlib import ExitStack
import numpy as np
import ml_dtypes
import concourse.bass as bass
import concourse.mybir as mybir
from concourse.bass_utils import run_bass_kernel_spmd

F32 = mybir.dt.float32
BF16 = mybir.dt.bfloat16
I32 = mybir.dt.int32
AF = mybir.ActivationFunctionType
ALU = mybir.AluOpType

NSEQ = 2
S_LEN = 4096
D = 1024
NH = 8
DH = 64
D_IN = 3080
D_FF = 2816
NJ = 22
ALPHA = 2.0 ** 0.25
LN_EPS = 1e-5
RMS_EPS = 1e-6
TWO_PI = 2.0 * math.pi
PI_SAFE = 3.1415925
C_QF, C_KF, C_VF, C_FA, C_QD, C_KD, C_VD = 0, 512, 1024, 1536, 1544, 2056, 2568
VW = 66
VROW = 8 * VW

SAME_ENGINE_SYNC = True
DEBUG = False
LIM = {"P_seq": NSEQ, "P_J": 8, "P_F": True, "P_parts": "qkdvf", "P_rope": True, "d_mode": 0, "A_seq": NSEQ, "A_heads": NH, "B_seq": NSEQ}


class Buf:
    __slots__ = ("w", "rs")

    def __init__(self):
        self.w = None
        self.rs = []


class Sched:
    ENG = ("pe", "act", "dve", "pool", "sp")

    def __init__(self, nc, stack):
        self.nc = nc
        self.ops = {e: [] for e in self.ENG}
        self.cnt = {}
        self.sem = {}
        for e in self.ENG:
            self.sem[e] = stack.enter_context(nc.semaphore("s_" + e))
            self.cnt[e] = 0
        self.dq = {}
        for q, n in (("sp", 16), ("pool", 16), ("act", 4)):
            keys = []
            for i in range(n):
                k = "d_%s%d" % (q, i)
                self.sem[k] = stack.enter_context(nc.semaphore("s_" + k))
                self.cnt[k] = 0
                keys.append(k)
            self.dq[q] = [keys, 0]
        self.seen = {e: {} for e in self.ENG}
        self.snap = {}

    def _collect(self, e, reads, writes, extra=()):
        need = {}

        def add(t):
            if t is None:
                return
            k, v = t
            if need.get(k, 0) < v:
                need[k] = v
        for b in reads:
            add(b.w)
        for b in writes:
            add(b.w)
            for t in b.rs:
                add(t)
        for t in extra:
            add(t)
        waits = []
        seen = self.seen[e]
        for k, v in need.items():
            if k == e and (e in ("pe", "sp") or not SAME_ENGINE_SYNC):
                continue
            if seen.get(k, 0) >= v:
                continue
            waits.append((k, v))
        for k, v in waits:
            sn = self.snap.get((k, v))
            if sn:
                for kk, vv in sn.items():
                    if seen.get(kk, 0) < vv:
                        seen[kk] = vv
            seen[k] = v
        return waits

    def _commit(self, ticket, e, reads, writes):
        self.snap[ticket] = dict(self.seen[e])
        for b in reads:
            b.rs.append(ticket)
            if len(b.rs) > 64:
                b.rs = b.rs[-48:]
        for b in writes:
            b.w = ticket
            b.rs = []

    def op(self, e, fn, reads=(), writes=(), inc=True):
        waits = self._collect(e, reads, writes)
        if inc:
            self.cnt[e] += 1
            ticket = (e, self.cnt[e])
            self.ops[e].append((waits, fn, (e, 1)))
        else:
            ticket = (e, self.cnt[e] + 1)
            self.ops[e].append((waits, fn, None))
        self._commit(ticket, e, reads, writes)
        return ticket

    def dma(self, q, out, in_, reads=(), writes=(), **kw):
        keys, rr = self.dq[q]
        k = keys[rr % len(keys)]
        self.dq[q][1] = rr + 1
        prev = (k, self.cnt[k]) if self.cnt[k] else None
        waits = self._collect(q, reads, writes, extra=(prev,) if prev else ())
        self.cnt[k] += 16
        ticket = (k, self.cnt[k])

        def fn(eng, out=out, in_=in_, kw=kw):
            return eng.dma_start(out=out, in_=in_, **kw)
        self.ops[q].append((waits, fn, (k, 16)))
        self._commit(ticket, q, reads, writes)
        return ticket

    def barrier(self):
        tickets = [(k, v) for k, v in self.cnt.items() if v > 0]
        for e in self.ENG:
            waits = self._collect(e, (), (), extra=tickets)
            self.ops[e].append((waits, None, None))

    def check(self):
        val = {k: 0 for k in self.sem}
        pc = {e: 0 for e in self.ENG}
        progress = True
        while progress:
            progress = False
            for e in self.ENG:
                ops = self.ops[e]
                while pc[e] < len(ops):
                    waits, fn, inc = ops[pc[e]]
                    if any(val[k] < v for k, v in waits):
                        break
                    if inc is not None:
                        val[inc[0]] += inc[1]
                    pc[e] += 1
                    progress = True
        stuck = {e: (pc[e], len(self.ops[e])) for e in self.ENG if pc[e] < len(self.ops[e])}
        for e, (p, n) in stuck.items():
            waits = self.ops[e][p][0]
            print("STUCK", e, p, n, [(k, v, val[k]) for k, v in waits if val[k] < v])
        print("check: ops per engine", {e: len(self.ops[e]) for e in self.ENG}, "stuck:", bool(stuck))
        return not stuck

    def emit(self):
        nc = self.nc
        if not self.check():
            raise RuntimeError("scheduler deadlock")
        with nc.Block() as block:
            def make(e):
                def body(eng):
                    for waits, fn, inc in self.ops[e]:
                        for k, v in waits:
                            eng.wait_ge(self.sem[k], v)
                        if fn is None:
                            continue
                        ins = fn(eng)
                        if inc is not None:
                            ins.then_inc(self.sem[inc[0]], inc[1])
                return body
            block.tensor(make("pe"))
            block.scalar(make("act"))
            block.vector(make("dve"))
            block.gpsimd(make("pool"))
            block.sync(make("sp"))


class Arena:
    def __init__(self, ap, nwords):
        self.ap = ap
        self.n = nwords
        self.top = 0
        self.base = 0

    def words(self, n):
        n = (n + 15) // 16 * 16
        off = self.top
        self.top += n
        assert self.top <= getattr(self, "limit", self.n), "SBUF arena overflow %d > %d" % (self.top, getattr(self, "limit", self.n))
        return off

    def f32(self, cols):
        off = self.words(cols)
        return self.ap[:, off:off + cols]

    def bf16(self, cols):
        nw = (cols + 1) // 2
        off = self.words(nw)
        return self.ap[:, off:off + nw].bitcast(BF16)[:, 0:cols]

    def i32(self, cols):
        off = self.words(cols)
        return self.ap[:, off:off + cols].bitcast(I32)

    def at_bf16(self, off_words, cols):
        nw = (cols + 1) // 2
        assert off_words + nw <= self.n
        return self.ap[:, off_words:off_words + nw].bitcast(BF16)[:, 0:cols]

    def mark(self):
        self.base = self.top

    def reset(self):
        self.top = self.base


def build_program(stop=None, debug=False):
    global DEBUG
    DEBUG = debug
    nc = bass.Bass("TRN2", target_bir_lowering=False)

    def din(name, shape, dt=F32):
        return nc.dram_tensor(name, list(shape), dt, kind="ExternalInput").ap()

    def dscr(name, shape, dt):
        return nc.dram_tensor(name, list(shape), dt, kind="ExternalOutput" if DEBUG else "Internal").ap()

    xT_d = din("xT", [NSEQ, D, S_LEN])
    x_d = din("x", [NSEQ, S_LEN, D])
    c_d = din("c_arr", [128, 16])
    pos_d = din("pos_arr", [NSEQ, 128, 32], I32)
    wada_d = din("w_ada", [D, 6 * D])
    bada_fm_d = din("b_ada_fm", [128, 48])
    bada_row_d = din("b_ada_row", [1, 6 * D])
    win_d = din("w_in", [D, D_IN])
    bfg_d = din("b_fgate", [1, 8])
    gn_d = din("gn", [64, 16])
    wout_d = din("w_out", [D, D])
    ln1g_d = din("ln1_g", [1, D])
    ln1b_d = din("ln1_b", [1, D])
    wup_d = din("w_up", [D, 2 * D_FF])
    cw_d = din("conv_wf", [128, 3 * 44])
    cb_d = din("conv_bf", [128, 44])
    wdn_d = din("w_down", [D_FF, D])
    ln2g_d = din("ln2_g", [1, D])
    ln2b_d = din("ln2_b", [1, D])
    ident_d = din("ident", [128, 128])
    tri_d = din("tri", [128, 128])
    triT_d = din("triT", [128, 128])
    freq_d = din("freq", [128, 8])
    out_d = nc.dram_tensor("out", [NSEQ, S_LEN, D], F32, kind="ExternalOutput").ap()

    QTF = dscr("QTF", [NSEQ, 512, S_LEN], BF16)
    KTF = dscr("KTF", [NSEQ, 512, S_LEN], BF16)
    QTD = dscr("QTD", [NSEQ, 512, S_LEN], BF16)
    KTD = dscr("KTD", [NSEQ, 512, S_LEN], BF16)
    VF = dscr("VF", [NSEQ, S_LEN, VROW], BF16)
    VD = dscr("VD", [NSEQ, S_LEN, VROW], BF16)
    FR = dscr("FR", [NSEQ, 24, S_LEN], BF16)
    MT = dscr("MT", [NSEQ, D, S_LEN], BF16)
    X1 = dscr("X1", [NSEQ, S_LEN, D], F32)
    H2T = dscr("H2T", [NSEQ, D, S_LEN + 2], BF16)
    ACTT = dscr("ACTT", [NSEQ, D_FF, S_LEN], BF16)

    with ExitStack() as st:
        S = Sched(nc, st)
        NW = 52000
        arena_t = st.enter_context(nc.sbuf_tensor("arena", [128, NW], F32))
        A = Arena(arena_t, NW)
        PS = st.enter_context(nc.psum_tensor("ps", [128, 4096], F32))

        def bank(i):
            return PS[:, i * 512:(i + 1) * 512]
        bPS = [Buf() for _ in range(8)]

        ident32 = A.f32(128)
        identb = A.bf16(128)
        trib = A.bf16(128)
        maskD2 = A.bf16(512)
        tri32 = A.f32(128)
        maskneg = A.bf16(128)
        ones32 = A.f32(128)
        freq = A.f32(8)
        gain = A.f32(16)
        bfg = A.f32(8)
        cw = A.f32(132)
        cb = A.f32(44)
        onesE = A.bf16(64)
        adaT = A.f32(96)
        sc1a = A.f32(16)
        sha = A.f32(16)
        sc1f = A.f32(16)
        shf = A.f32(16)
        gab = [A.f32(1024) for _ in range(NSEQ)]
        gfb = [A.f32(1024) for _ in range(NSEQ)]
        negF = [A.f32(256) for _ in range(NSEQ)]
        bConst = Buf()
        bAda = Buf()
        bNegF = [Buf() for _ in range(NSEQ)]
        bG = Buf()

        S.dma("sp", ident32, ident_d, writes=[bConst])
        S.dma("sp", tri32, tri_d, writes=[bConst])
        S.dma("sp", freq, freq_d, writes=[bConst])
        S.dma("sp", gain[0:64, :], gn_d, writes=[bConst])
        S.dma("sp", bfg, bfg_d.broadcast_to([128, 8]), writes=[bConst])
        S.dma("sp", cw, cw_d, writes=[bConst])
        S.dma("sp", cb, cb_d, writes=[bConst])
        S.dma("pool", identb, ident_d, writes=[bConst])
        S.dma("pool", trib, tri_d, writes=[bConst])
        md = maskD2.rearrange("p (j h q) -> p j h q", j=2, h=2)
        for j in range(2):
            S.dma("pool", md[:, j, 0, :], triT_d, writes=[bConst])
            S.dma("pool", md[:, j, 1, :], tri_d, writes=[bConst])
        S.op("dve", lambda e: e.memset(ones32, 1.0), writes=[bConst])
        S.op("dve", lambda e: e.tensor_scalar(out=maskneg, in0=tri32, scalar1=-1.0, scalar2=30000.0, op0=ALU.add, op1=ALU.mult), reads=[bConst], writes=[bConst])
        S.op("dve", lambda e: e.memset(onesE[0:64, :], 1.0), writes=[bConst])
        S.op("dve", lambda e: e.memset(onesE[64:65, :], 64.0 * RMS_EPS), writes=[bConst])
        A.mark()

        A.reset()
        WIN_OFF = NW - 12320
        WUP_OFF = NW - 22528
        WDN_OFF = WUP_OFF - 11264
        A.limit = WIN_OFF
        win = A.at_bf16(WIN_OFF, 8 * D_IN)
        winv = win.rearrange("p (c f) -> p c f", c=8)
        bWin = Buf()
        win_dv = win_d.rearrange("(c p) f -> p c f", p=128)
        for c in range(8):
            for hf in range(2):
                S.dma("pool", winv[:, c, hf * 1540:(hf + 1) * 1540], win_dv[:, c, hf * 1540:(hf + 1) * 1540], writes=[bWin])
        c_sb = A.f32(16)
        sc = A.f32(16)
        screp = [A.f32(1024) for _ in range(NSEQ)]
        badafm = A.f32(48)
        badarow = A.f32(2048)
        wg = [A.f32(8192) for _ in range(2)]
        bwg = [Buf(), Buf()]
        bsc = Buf()
        S.dma("sp", c_sb, c_d, writes=[bsc])
        S.dma("sp", badafm, bada_fm_d, writes=[bsc])
        S.dma("sp", badarow[:, 0:1024], bada_row_d[:, 2048:3072].broadcast_to([128, 1024]), writes=[bsc])
        S.dma("sp", badarow[:, 1024:2048], bada_row_d[:, 5120:6144].broadcast_to([128, 1024]), writes=[bsc])
        S.op("act", lambda e: e.activation(out=sc, in_=c_sb, func=AF.Silu), reads=[bsc], writes=[bsc])
        scv = sc.rearrange("p (c b) -> p c b", b=2)
        for s in range(NSEQ):
            rv = screp[s].rearrange("p (c m) -> p c m", m=128)
            for c in range(8):
                S.op("dve", lambda e, o=rv[:, c, :], sca=scv[:, c, s:s + 1]: e.tensor_scalar(
                    out=o, in0=ones32, scalar1=sca, scalar2=None, op0=ALU.mult), reads=[bsc, bConst], writes=[bsc])
        wada_v = wada_d.rearrange("(c p) f -> p c f", p=128)
        adaps = bank(7)[:, 0:96]
        for gi in range(6):
            wt = wg[gi % 2]
            wv = wt.rearrange("p (c f) -> p c f", c=8)
            for c in range(8):
                S.dma("sp", wv[:, c, :], wada_v[:, c, gi * 1024:(gi + 1) * 1024], writes=[bwg[gi % 2]])
            for fc in range(8):
                col = (gi * 8 + fc) * 2
                for c in range(8):
                    S.op("pe", lambda e, o=adaps[:, col:col + 2], l=wv[:, c, fc * 128:(fc + 1) * 128], r=scv[:, c, :], c=c:
                         e.matmul(o, lhsT=l, rhs=r, start=(c == 0), stop=(c == 7)),
                         reads=[bwg[gi % 2], bsc], writes=[bPS[7]], inc=(c == 7))
            if gi in (2, 5):
                dst = gab if gi == 2 else gfb
                brow = badarow[:, 0:1024] if gi == 2 else badarow[:, 1024:2048]
                for s in range(NSEQ):
                    rv = screp[s].rearrange("p (c m) -> p c m", m=128)
                    for half in range(2):
                        pb = bank(half)
                        for c in range(8):
                            S.op("pe", lambda e, o=pb, l=rv[:, c, :], r=wv[:, c, half * 512:(half + 1) * 512], c=c:
                                 e.matmul(o, lhsT=l, rhs=r, start=(c == 0), stop=(c == 7)),
                                 reads=[bwg[gi % 2], bsc], writes=[bPS[half]], inc=(c == 7))
                        S.op("dve", lambda e, o=dst[s][:, half * 512:(half + 1) * 512], i0=pb, i1=brow[:, half * 512:(half + 1) * 512]:
                             e.tensor_tensor(out=o, in0=i0, in1=i1, op=ALU.add), reads=[bPS[half], bsc], writes=[bG])
        adaTv = adaT.rearrange("p (k b) -> p k b", b=2)
        adapv = adaps.rearrange("p (k b) -> p k b", b=2)
        for b in range(2):
            S.op("dve", lambda e, o=adaTv[:, :, b], i0=adapv[:, :, b]: e.tensor_tensor(out=o, in0=i0, in1=badafm, op=ALU.add),
                 reads=[bPS[7], bsc], writes=[bAda])
        for s in range(NSEQ):
            S.op("dve", lambda e, o=sc1a[:, s * 8:(s + 1) * 8], i=adaTv[:, 8:16, s]: e.tensor_scalar(
                out=o, in0=i, scalar1=1.0, scalar2=None, op0=ALU.add), reads=[bAda], writes=[bAda])
            S.op("dve", lambda e, o=sha[:, s * 8:(s + 1) * 8], i=adaTv[:, 0:8, s]: e.tensor_copy(out=o, in_=i), reads=[bAda], writes=[bAda])
            S.op("dve", lambda e, o=sc1f[:, s * 8:(s + 1) * 8], i=adaTv[:, 32:40, s]: e.tensor_scalar(
                out=o, in0=i, scalar1=1.0, scalar2=None, op0=ALU.add), reads=[bAda], writes=[bAda])
            S.op("dve", lambda e, o=shf[:, s * 8:(s + 1) * 8], i=adaTv[:, 24:32, s]: e.tensor_copy(out=o, in_=i), reads=[bAda], writes=[bAda])
        S.barrier()

        if stop == 'ada':
            S.emit()
            return nc
        A.reset()
        A.limit = WIN_OFF
        xst = [A.f32(512) for _ in range(4)]
        bxst = [Buf() for _ in range(4)]
        hT = [A.bf16(8 * 512) for _ in range(2)]
        bhT = [Buf() for _ in range(2)]
        zb = [A.bf16(512) for _ in range(4)]
        bzb = [Buf() for _ in range(4)]
        tstage = {}
        for nm in ("QF", "KF", "QD", "KD"):
            tstage[nm] = ([A.bf16(4 * 512) for _ in range(2)], [Buf() for _ in range(2)])
        vst = [A.bf16(VROW) for _ in range(4)]
        bvst = [Buf() for _ in range(4)]
        rtmp = [A.f32(64) for _ in range(4)]
        z32 = [A.f32(512) for _ in range(2)]
        bz32 = [Buf(), Buf()]
        brtmp = [Buf() for _ in range(4)]
        posi = A.i32(32)
        posf = A.f32(32)
        ang = A.f32(256)
        kfl = A.f32(256)
        kin = A.i32(256)
        rr = A.f32(256)
        rc = A.f32(256)
        mm = A.f32(256)
        cosq = A.f32(256)
        sinq = A.f32(256)
        cosk = A.f32(256)
        sink = A.f32(256)
        tab8 = {}
        for nm_ in ("cosk", "sink"):
            tab8[nm_] = A.f32(32 * 64)
        LF = A.f32(256)
        fax = A.f32(8)
        faa = A.f32(8)
        carry = A.f32(256)
        Fm = A.f32(256)
        Fhi = A.bf16(256)
        Fr1 = A.f32(256)
        Fmid = A.bf16(256)
        Fr2 = A.f32(256)
        Flo = A.bf16(256)
        Fs = A.bf16(32 * 24)
        FsT = A.bf16(S_LEN)
        bRope = Buf()
        bLF = Buf()
        bF = Buf()
        for i in range(4):
            S.op("dve", lambda e, o=vst[i]: e.memset(o, 1.0), writes=[bvst[i]])

        LFv = LF.rearrange("p (n h) -> p n h", h=8)
        R = [bRope]

        def rope_tables(s):
            S.dma("sp", posi, pos_d[s], writes=[bRope])
            S.op("dve", lambda e: e.tensor_copy(out=posf, in_=posi), reads=[bRope], writes=[bRope])
            angv = ang.rearrange("p (n j) -> p n j", j=8)
            for n in range(32):
                S.op("dve", lambda e, o=angv[:, n, :], sca=posf[:, n:n + 1]: e.tensor_scalar(
                    out=o, in0=freq, scalar1=sca, scalar2=None, op0=ALU.mult), reads=[bRope, bConst], writes=[bRope])
            S.op("dve", lambda e: e.tensor_scalar(out=kfl, in0=ang, scalar1=1.0 / TWO_PI, scalar2=None, op0=ALU.mult), reads=R, writes=R)
            S.op("dve", lambda e: e.tensor_copy(out=kin, in_=kfl), reads=R, writes=R)
            S.op("dve", lambda e: e.tensor_copy(out=kfl, in_=kin), reads=R, writes=R)
            S.op("dve", lambda e: e.scalar_tensor_tensor(out=rr, in0=kfl, scalar=-6.28125, in1=ang, op0=ALU.mult, op1=ALU.add), reads=R, writes=R)
            S.op("dve", lambda e: e.scalar_tensor_tensor(out=rr, in0=kfl, scalar=-(TWO_PI - 6.28125), in1=rr, op0=ALU.mult, op1=ALU.add), reads=R, writes=R)
            S.op("dve", lambda e: e.tensor_scalar(out=rc, in0=rr, scalar1=math.pi / 2, scalar2=None, op0=ALU.add), reads=R, writes=R)
            S.op("dve", lambda e: e.tensor_scalar(out=mm, in0=rc, scalar1=math.pi, scalar2=None, op0=ALU.is_gt), reads=R, writes=R)
            S.op("dve", lambda e: e.scalar_tensor_tensor(out=rc, in0=mm, scalar=-TWO_PI, in1=rc, op0=ALU.mult, op1=ALU.add), reads=R, writes=R)
            for t in (rr, rc):
                S.op("dve", lambda e, t=t: e.tensor_scalar(out=t, in0=t, scalar1=-PI_SAFE, scalar2=PI_SAFE, op0=ALU.max, op1=ALU.min), reads=R, writes=R)
            S.op("act", lambda e: e.activation(out=sink, in_=rr, func=AF.Sin), reads=R, writes=R)
            S.op("act", lambda e: e.activation(out=cosk, in_=rc, func=AF.Sin), reads=R, writes=R)
            for nm_, t_ in (("cosk", cosk), ("sink", sink)):
                t8 = tab8[nm_].rearrange("p (n h j) -> p n h j", h=8, j=8)
                for hh in range(8):
                    S.op("dve", lambda e, o=t8[:, :, hh, :], i_=t_.rearrange("p (n j) -> p n j", j=8): e.tensor_copy(out=o, in_=i_), reads=R, writes=R)

        def f_stage(s):
            fps = bank(5)[:, 0:256]
            rps = bank(5)[:, 256:512]
            S.op("pe", lambda e: e.matmul(fps, lhsT=tri32, rhs=LF, start=True, stop=True), reads=[bLF, bConst], writes=[bPS[5]])
            S.op("pe", lambda e: e.matmul(rps, lhsT=ones32, rhs=LF, start=True, stop=True), reads=[bLF, bConst], writes=[bPS[5]])
            carv = carry.rearrange("p (n h) -> p n h", h=8)
            rpv = rps.rearrange("p (n h) -> p n h", h=8)
            S.op("dve", lambda e: e.memset(carv[:, 0, :], 0.0), writes=[bF])
            for n in range(1, 32):
                S.op("dve", lambda e, o=carv[:, n, :], a=rpv[:, n - 1, :], b=carv[:, n - 1, :]: e.tensor_tensor(out=o, in0=a, in1=b, op=ALU.add),
                     reads=[bPS[5], bF], writes=[bF])
            S.op("dve", lambda e: e.tensor_tensor(out=Fm, in0=fps, in1=carry, op=ALU.add), reads=[bPS[5], bF], writes=[bF])
            S.op("dve", lambda e, o=negF[s]: e.tensor_scalar(out=o, in0=Fm, scalar1=-1.0, scalar2=None, op0=ALU.mult), reads=[bF], writes=[bNegF[s]])
            S.op("dve", lambda e: e.tensor_copy(out=Fhi, in_=Fm), reads=[bF], writes=[bF])
            S.op("dve", lambda e: e.tensor_tensor(out=Fr1, in0=Fm, in1=Fhi, op=ALU.subtract), reads=[bF], writes=[bF])
            S.op("dve", lambda e: e.tensor_copy(out=Fmid, in_=Fr1), reads=[bF], writes=[bF])
            S.op("dve", lambda e: e.tensor_tensor(out=Fr2, in0=Fr1, in1=Fmid, op=ALU.subtract), reads=[bF], writes=[bF])
            S.op("dve", lambda e: e.tensor_copy(out=Flo, in_=Fr2), reads=[bF], writes=[bF])
            Fsv = Fs.rearrange("p (n j h) -> p n j h", j=3, h=8)
            for j, src_ in enumerate((Fhi, Fmid, Flo)):
                S.op("dve", lambda e, o=Fsv[:, :, j, :], i_=src_.rearrange("p (n h) -> p n h", h=8): e.tensor_copy(out=o, in_=i_), reads=[bF], writes=[bF])
            Fs2 = Fs.rearrange("p (n k) -> p n k", k=24)
            for g4 in range(4):
                tbi = 6 + (g4 % 2)
                tb = bank(tbi).bitcast(BF16).rearrange("p (g t) -> p g t", t=128)
                for k in range(8):
                    n = g4 * 8 + k
                    S.op("pe", lambda e, o=tb[0:24, k, :], i_=Fs2[:, n, :]: e.transpose(o, i_, identb), reads=[bF, bConst], writes=[bPS[tbi]])
                S.op("act", lambda e, o=FsT[0:24, g4 * 1024:(g4 + 1) * 1024], i_=bank(tbi).bitcast(BF16)[0:24, :]: e.activation(out=o, in_=i_, func=AF.Copy),
                     reads=[bPS[tbi]], writes=[bF])
            S.dma("pool", FR[s], FsT[0:24, :], reads=[bF])

        GRPS = [g_ for g_ in ("QF", "KF", "QD", "KD", "VF", "VD", "FA")
                if {"QF": "q", "KF": "q", "QD": "d", "KD": "d", "VF": "v", "VD": "v", "FA": "f"}[g_] in LIM["P_parts"]]
        GINFO = {"QF": (C_QF, 512, 0.125, QTF), "KF": (C_KF, 512, 1.0, KTF), "QD": (C_QD, 512, 0.125, QTD), "KD": (C_KD, 512, 1.0, KTD),
                 "VF": (C_VF, 512, 1.0, VF), "VD": (C_VD, 512, 1.0, VD), "FA": (C_FA, 8, 1.0, None)}
        jtiles = [(s, J) for s in range(LIM["P_seq"]) for J in range(LIM["P_J"])]
        pitems = [(ji, i, g_) for ji in range(len(jtiles)) for i in range(4) for g_ in GRPS]
        xcount = [0]

        def make_hT(ji):
            s, J = jtiles[ji]
            hb = ji % 2
            hTv = hT[hb].rearrange("p (c t) -> p c t", c=8)
            for c in range(8):
                xs = xcount[0] % 4
                xcount[0] += 1
                S.dma("sp", xst[xs], xT_d[s, c * 128:(c + 1) * 128, J * 512:(J + 1) * 512], writes=[bxst[xs]])
                S.op("act", lambda e, o=hTv[:, c, :], i=xst[xs], sca=sc1a[:, s * 8 + c:s * 8 + c + 1], bi=sha[:, s * 8 + c:s * 8 + c + 1]:
                     e.activation(out=o, in_=i, func=AF.Identity, bias=bi, scale=sca),
                     reads=[bxst[xs], bAda], writes=[bhT[hb]])

        def p_head(k):
            ji, i, g_ = pitems[k]
            s, J = jtiles[ji]
            if i == 0 and g_ == GRPS[0]:
                if ji == 0:
                    make_hT(0)
                if ji + 1 < len(jtiles):
                    make_hT(ji + 1)
            hb = ji % 2
            hTv = hT[hb].rearrange("p (c t) -> p c t", c=8)
            tok = slice(i * 128, (i + 1) * 128)
            col0, ncols, scl, dram = GINFO[g_]
            pbi = k % 5
            pb = bank(pbi)[:, 0:ncols]
            for c in range(8):
                S.op("pe", lambda e, o=pb, l=hTv[:, c, tok], r=winv[:, c, col0:col0 + ncols], c=c:
                     e.matmul(o, lhsT=l, rhs=r, start=(c == 0), stop=(c == 7)),
                     reads=[bhT[hb], bWin], writes=[bPS[pbi]], inc=(c == 7))

        rope_done = set()
        tcnt = [0]
        vcnt = [0]
        rcnt = [0]

        def p_transposes(k, nm, zbuf, bz, dram, s, J, i):
            stg, bst_ = tstage[nm]
            sb = (s * 8 + J) % 2
            tok = slice(i * 128, (i + 1) * 128)
            tbi = 6 + (tcnt[0] % 2)
            tcnt[0] += 1
            tb = bank(tbi).bitcast(BF16).rearrange("p (g t) -> p g t", t=128)
            for g in range(4):
                S.op("pe", lambda e, o=tb[:, g, :], i_=zbuf[:, g * 128:(g + 1) * 128]: e.transpose(o, i_, identb),
                     reads=[bz, bConst], writes=[bPS[tbi]])
            sv = stg[sb].rearrange("p (g t) -> p g t", g=4)
            S.op("act", lambda e, o=sv[:, :, tok], i_=tb[:, 0:4, :]: e.activation(out=o, in_=i_, func=AF.Copy),
                 reads=[bPS[tbi]], writes=[bst_[sb]])
            if i == 3:
                for g in range(4):
                    S.dma("pool", dram[s, g * 128:(g + 1) * 128, J * 512:(J + 1) * 512], sv[:, g, :], reads=[bst_[sb]])

        def p_tail(k):
            ji, i, g_ = pitems[k]
            s, J = jtiles[ji]
            n = J * 4 + i
            col0, ncols, scl, dram = GINFO[g_]
            pbi = k % 5
            pb = bank(pbi)[:, 0:ncols]
            zi = k % 4
            if g_ in ("QF", "KF"):
                S.op("act", lambda e, o=zb[zi], i_=pb, scl=scl: e.activation(out=o, in_=i_, func=AF.Copy, scale=scl),
                     reads=[bPS[pbi]], writes=[bzb[zi]])
                p_transposes(k, g_, zb[zi], bzb[zi], dram, s, J, i)
            elif g_ in ("QD", "KD"):
                if s not in rope_done:
                    rope_done.add(s)
                    rope_tables(s)
                ct, stb = tab8["cosk"], tab8["sink"]
                zv = zb[zi].rearrange("p (h d) -> p h d", d=64)
                z3i = rcnt[0] % 2
                rcnt[0] += 1
                z3 = z32[z3i]
                S.op("act", lambda e, o=z3, i_=pb, scl=scl: e.activation(out=o, in_=i_, func=AF.Copy, scale=scl),
                     reads=[bPS[pbi]], writes=[bz32[z3i]])
                pv = z3.rearrange("p (h d) -> p h d", d=64)
                cbv = ct.rearrange("p (n h j) -> p n h j", h=8, j=8)[:, n, :, :]
                sbv = stb.rearrange("p (n h j) -> p n h j", h=8, j=8)[:, n, :, :]
                t1 = pv[:, :, 0:8]
                t2 = pv[:, :, 8:16]
                S.op("act", lambda e, o=zv[:, :, 16:64], i_=pv[:, :, 16:64]: e.activation(out=o, in_=i_, func=AF.Copy),
                     reads=[bz32[z3i]], writes=[bzb[zi]])
                tA = rtmp[0].rearrange("p (h d) -> p h d", d=8)
                tB = rtmp[1].rearrange("p (h d) -> p h d", d=8)
                tC = rtmp[2].rearrange("p (h d) -> p h d", d=8)
                tD = rtmp[3].rearrange("p (h d) -> p h d", d=8)
                RD = [bz32[z3i], bRope]
                S.op("dve", lambda e, o=tA, a=t1, b=cbv: e.tensor_tensor(out=o, in0=a, in1=b, op=ALU.mult), reads=RD, writes=[brtmp[0]])
                S.op("dve", lambda e, o=tB, a=t2, b=sbv: e.tensor_tensor(out=o, in0=a, in1=b, op=ALU.mult), reads=RD, writes=[brtmp[1]])
                S.op("dve", lambda e, o=tC, a=t2, b=cbv: e.tensor_tensor(out=o, in0=a, in1=b, op=ALU.mult), reads=RD, writes=[brtmp[2]])
                S.op("dve", lambda e, o=tD, a=t1, b=sbv: e.tensor_tensor(out=o, in0=a, in1=b, op=ALU.mult), reads=RD, writes=[brtmp[3]])
                S.op("dve", lambda e, o=zv[:, :, 0:8], a=tA, b=tB: e.tensor_tensor(out=o, in0=a, in1=b, op=ALU.subtract),
                     reads=[brtmp[0], brtmp[1]], writes=[bzb[zi]])
                S.op("dve", lambda e, o=zv[:, :, 8:16], a=tC, b=tD: e.tensor_tensor(out=o, in0=a, in1=b, op=ALU.add),
                     reads=[brtmp[2], brtmp[3]], writes=[bzb[zi]])
                p_transposes(k, g_, zb[zi], bzb[zi], dram, s, J, i)
            elif g_ in ("VF", "VD"):
                vi = vcnt[0] % 4
                vcnt[0] += 1
                vv = vst[vi].rearrange("p (h e) -> p h e", e=VW)
                S.op("dve", lambda e, o=vv[:, :, 0:64], i_=pb.rearrange("p (h d) -> p h d", d=64): e.tensor_copy(out=o, in_=i_),
                     reads=[bPS[pbi]], writes=[bvst[vi]])
                S.dma("pool", dram[s, n * 128:(n + 1) * 128, :], vst[vi], reads=[bvst[vi]])
            else:
                S.op("dve", lambda e, i_=pb: e.tensor_tensor(out=fax, in0=i_, in1=bfg, op=ALU.add), reads=[bPS[pbi], bConst], writes=[bLF])
                S.op("dve", lambda e: e.scalar_tensor_tensor(out=faa, in0=fax, scalar=-1.0, in1=fax, op0=ALU.mult, op1=ALU.max), reads=[bLF], writes=[bLF])
                S.op("act", lambda e: e.activation(out=faa, in_=faa, func=AF.Exp, scale=-1.0), reads=[bLF], writes=[bLF])
                S.op("act", lambda e: e.activation(out=faa, in_=faa, func=AF.Ln, bias=1.0, scale=1.0), reads=[bLF], writes=[bLF])
                S.op("dve", lambda e, o=LFv[:, n, :]: e.scalar_tensor_tensor(out=o, in0=fax, scalar=0.0, in1=faa, op0=ALU.min, op1=ALU.subtract),
                     reads=[bLF], writes=[bLF])
            if LIM["P_F"] and (k + 1 == len(pitems) or jtiles[pitems[k + 1][0]][0] != s):
                f_stage(s)

        LA = 3
        for k in range(len(pitems) + LA):
            if k < len(pitems):
                p_head(k)
            if k >= LA:
                p_tail(k - LA)
        S.barrier()

        if stop == 'P':
            S.emit()
            return nc
        A.limit = NW
        from collections import deque

        def finalize_ops(src, bsrc, h_glob, s, qc, work, wi, ssbanks):
            sq, lnv, rinv, mg, bsq, bln, brv, bmg = work
            k = wi % 2
            ssi = ssbanks[wi % len(ssbanks)]
            ssb = bank(ssi)

            def fa():
                S.op("act", lambda e, o=sq[k][0:65, :], i_=src[0:65, :]: e.activation(out=o, in_=i_, func=AF.Square), reads=[bsrc], writes=[bsq[k]])
                S.op("pe", lambda e, o=ssb[0:64, :], r=sq[k][0:65, :]: e.matmul(o, lhsT=onesE[0:65, 0:64], rhs=r, start=True, stop=True),
                     reads=[bsq[k], bConst], writes=[bPS[ssi]])

            def fb():
                S.op("act", lambda e, o=lnv[k][0:64, :], i_=ssb[0:64, :]: e.activation(out=o, in_=i_, func=AF.Ln, scale=1.0 / 64.0), reads=[bPS[ssi]], writes=[bln[k]])
                S.op("act", lambda e, o=rinv[k][0:64, :], i_=lnv[k][0:64, :]: e.activation(out=o, in_=i_, func=AF.Exp, scale=-0.5), reads=[bln[k]], writes=[brv[k]])
                S.op("dve", lambda e, o=mg[k][0:64, :], a=src[0:64, :], g=gain[0:64, h_glob:h_glob + 1], b=rinv[k][0:64, :]:
                     e.scalar_tensor_tensor(out=o, in0=a, scalar=g, in1=b, op0=ALU.mult, op1=ALU.mult),
                     reads=[bsrc, brv[k], bConst], writes=[bmg[k]])
                S.dma("pool", MT[s, h_glob * 64:(h_glob + 1) * 64, qc * 512:(qc + 1) * 512], mg[k][0:64, :], reads=[bmg[k]])
            return fa, fb

        def alloc_work():
            sq = [A.bf16(512) for _ in range(2)]
            lnv = [A.f32(512) for _ in range(2)]
            rinv = [A.f32(512) for _ in range(2)]
            mg = [A.bf16(512) for _ in range(2)]
            return (sq, lnv, rinv, mg, [Buf(), Buf()], [Buf(), Buf()], [Buf(), Buf()], [Buf(), Buf()])

        A.reset()
        Vf = [A.bf16(32 * VROW) for _ in range(2)]
        bVf = [Buf(), Buf()]
        QA = [A.bf16(S_LEN) for _ in range(2)]
        KA = [A.bf16(S_LEN) for _ in range(2)]
        bQA = [Buf(), Buf()]
        bKA = [Buf(), Buf()]
        PT = [A.bf16(512) for _ in range(4)]
        bPT = [Buf() for _ in range(4)]
        work = alloc_work()
        for k in range(2):
            S.op("dve", lambda e, o=KA[k][64:67, :]: e.memset(o, 1.0), writes=[bKA[k]])
        blocks = []
        hcount = 0
        ocount = 0
        for s in range(LIM["A_seq"]):
            for h in range(LIM["A_heads"]):
                for qc in range(8):
                    nk = 4 * qc + 4
                    for kc in range(nk):
                        blocks.append((s, h, qc, kc, nk, hcount % 2, 3 + (ocount % 2)))
                    ocount += 1
                hcount += 1
        loaded_s = set()
        loaded_h = set()
        pending = deque()
        wi = 0

        def fox_head(i):
            s, h, qc, kc, nk, hb, ob = blocks[i]
            if s not in loaded_s:
                loaded_s.add(s)
                vsrc = VF[s].rearrange("(n p) e -> p n e", p=128)
                Vf3 = Vf[s % 2].rearrange("p (n e) -> p n e", e=VROW)
                for q4 in range(4):
                    S.dma("sp", Vf3[:, q4 * 8:(q4 + 1) * 8, :], vsrc[:, q4 * 8:(q4 + 1) * 8, :], writes=[bVf[s % 2]])
            if (s, h) not in loaded_h:
                loaded_h.add((s, h))
                S.dma("sp", QA[hb][0:64, :], QTF[s, h * 64:(h + 1) * 64, :], writes=[bQA[hb]])
                S.dma("sp", QA[hb][64:67, :], FR[s].rearrange("(j h) t -> h j t", h=8)[h], writes=[bQA[hb]])
                S.dma("sp", KA[hb][0:64, :], KTF[s, h * 64:(h + 1) * 64, :], writes=[bKA[hb]])
            j = kc - 4 * qc
            c0 = 128 * j if j > 0 else 0
            ncol = 512 - c0
            sb = i % 3
            STb = bank(sb)
            S.op("pe", lambda e, o=STb[:, 0:ncol], l=KA[hb][0:67, kc * 128:(kc + 1) * 128], r=QA[hb][0:67, qc * 512 + c0:(qc + 1) * 512], j=j:
                 e.matmul(o, lhsT=l, rhs=r, start=True, stop=(j < 0)), reads=[bQA[hb], bKA[hb]], writes=[bPS[sb]], inc=(j < 0))
            if j >= 0:
                S.op("pe", lambda e, o=STb[:, 0:128]: e.matmul(o, lhsT=identb, rhs=maskneg, start=False, stop=True),
                     reads=[bConst], writes=[bPS[sb]])

        def fox_tail(i):
            nonlocal wi
            s, h, qc, kc, nk, hb, ob = blocks[i]
            j = kc - 4 * qc
            c0 = 128 * j if j > 0 else 0
            ncol = 512 - c0
            sb = i % 3
            STb = bank(sb)
            pi = i % 4
            OT = bank(ob)
            nFv = negF[s].rearrange("p (n h) -> p n h", h=8)
            Vfv = Vf[s % 2].rearrange("p (n h e) -> p n h e", h=8, e=VW)
            S.op("act", lambda e, o=PT[pi][:, 0:ncol], i_=STb[:, 0:ncol], bi=nFv[:, kc, h:h + 1]:
                 e.activation(out=o, in_=i_, func=AF.Exp, bias=bi, scale=1.0), reads=[bPS[sb], bNegF[s]], writes=[bPT[pi]])
            S.op("pe", lambda e, o=OT[0:65, c0:512], l=Vfv[:, kc, h, 0:65], r=PT[pi][:, 0:ncol], kc=kc, nk=nk:
                 e.matmul(o, lhsT=l, rhs=r, start=(kc == 0), stop=(kc == nk - 1)), reads=[bVf[s % 2], bPT[pi]], writes=[bPS[ob]], inc=(kc == nk - 1))
            if kc == nk - 1:
                fa, fb = finalize_ops(OT, bPS[ob], h, s, qc, work, wi, (5, 6))
                wi += 1
                pending.append(fa)
                pending.append(fb)
            elif pending:
                pending.popleft()()

        LA = 2
        nb = len(blocks)
        for i in range(nb + LA):
            if i < nb:
                fox_head(i)
            if i >= LA:
                fox_tail(i - LA)
        while pending:
            pending.popleft()()
        S.barrier()

        if stop == 'fox':
            S.emit()
            return nc
        A.reset()
        V1 = A.bf16(32 * VROW)
        V4 = A.bf16(32 * VROW)
        V16 = A.bf16(32 * VROW)
        bVd = {1: Buf(), 4: Buf(), 16: Buf()}
        QDs = [A.bf16(S_LEN) for _ in range(2)]
        KDs = [A.bf16(S_LEN) for _ in range(2)]
        bQD = [Buf(), Buf()]
        bKD = [Buf(), Buf()]
        acc = [A.f32(S_LEN) for _ in range(2)]
        bacc = [Buf(), Buf()]
        PT = [A.bf16(512) for _ in range(4)]
        bPT = [Buf() for _ in range(4)]
        work = alloc_work()
        Vv = {1: V1.rearrange("p (n h e) -> p n h e", h=8, e=VW),
              4: V4.rearrange("p (n h e) -> p n h e", h=8, e=VW),
              16: V16.rearrange("p (n h e) -> p n h e", h=8, e=VW)}
        V3 = {1: V1.rearrange("p (n e) -> p n e", e=VROW), 4: V4.rearrange("p (n e) -> p n e", e=VROW),
              16: V16.rearrange("p (n e) -> p n e", e=VROW)}

        def blk(d, r, n):
            return r * (32 // d) + n

        def cols(d, r, n):
            st0 = d * 128 * n + r
            return slice(st0, st0 + d * 127 + 1, d)

        groups = []
        hcount = 0
        for s in range(LIM["A_seq"]):
            for h in range(LIM["A_heads"]):
                hb = hcount % 2
                hcount += 1
                ac = acc[hb]
                gl = []
                for g in range(8):
                    gl.append((1, [(0, 4 * g + jj) for jj in range(4)], ac[0:65, 512 * g:512 * (g + 1)].rearrange("p (j i) -> p j i", j=4)))
                for n in range(8):
                    gl.append((4, [(r, n) for r in range(4)], ac[0:65, 512 * n:512 * (n + 1)].rearrange("p (i r) -> p r i", r=4)))
                for n in range(2):
                    for r0 in range(0, 16, 4):
                        gl.append((16, [(r0 + jj, n) for jj in range(4)],
                                   ac[0:65, 2048 * n:2048 * (n + 1)].rearrange("p (i r) -> p r i", r=16)[:, r0:r0 + 4, :]))
                for gi_, (d, blocks_, accv) in enumerate(gl):
                    groups.append((s, h, hb, d, blocks_, accv, gi_ == len(gl) - 1))
        loaded_s = set()
        loaded_h = set()
        pending = deque()
        ginfo = {}

        def dil_head(gidx):
            s, h, hb, d, blocks_, accv, last = groups[gidx]
            if (s, h) not in loaded_h:
                loaded_h.add((s, h))
                S.dma("sp", QDs[hb][0:64, :], QTD[s, h * 64:(h + 1) * 64, :], writes=[bQD[hb]])
                S.dma("sp", KDs[hb][0:64, :], KTD[s, h * 64:(h + 1) * 64, :], writes=[bKD[hb]])
            if s not in loaded_s:
                loaded_s.add(s)
                vsrc = VD[s].rearrange("(n p) e -> p n e", p=128)
                for q4 in range(4):
                    S.dma("sp", V3[1][:, q4 * 8:(q4 + 1) * 8, :], vsrc[:, q4 * 8:(q4 + 1) * 8, :], writes=[bVd[1]])
                v4src = VD[s].rearrange("(n i r) e -> r i n e", n=8, i=128, r=4)
                for r in range(4):
                    S.dma("sp", V3[4][:, r * 8:(r + 1) * 8, :], v4src[r], writes=[bVd[4]])
                v16src = VD[s].rearrange("(n i r) e -> r i n e", n=2, i=128, r=16)
                for r in range(16):
                    S.dma("sp", V3[16][:, r * 2:(r + 1) * 2, :], v16src[r], writes=[bVd[16]])
            gp = gidx % 2
            info = []
            for half in range(2):
                sb = gp * 2 + half
                STv = bank(sb).rearrange("p (j h q) -> p j h q", j=2, h=2)
                pi = (gidx * 2 + half) % 4
                PTv = PT[pi].rearrange("p (j h q) -> p j h q", j=2, h=2)
                hasprev = []
                for jj in range(2):
                    r, n = blocks_[half * 2 + jj]
                    qcols = cols(d, r, n)
                    S.op("pe", lambda e, o=STv[:, jj, 1, :], l=KDs[hb][0:64, qcols], r_=QDs[hb][0:64, qcols]:
                         e.matmul(o, lhsT=l, rhs=r_, start=True, stop=True), reads=[bQD[hb], bKD[hb]], writes=[bPS[sb]],
                         inc=(jj == 1 and n < 1))
                    if n >= 1:
                        S.op("pe", lambda e, o=STv[:, jj, 0, :], l=KDs[hb][0:64, cols(d, r, n - 1)], r_=QDs[hb][0:64, qcols]:
                             e.matmul(o, lhsT=l, rhs=r_, start=True, stop=True), reads=[bQD[hb], bKD[hb]], writes=[bPS[sb]],
                             inc=(jj == 1))
                    hasprev.append(n >= 1)
                if all(hasprev):
                    S.op("act", lambda e, o=PT[pi], i_=bank(sb): e.activation(out=o, in_=i_, func=AF.Exp), reads=[bPS[sb]], writes=[bPT[pi]])
                    S.op("pool" if half == 0 else "dve", lambda e, o=PT[pi]: e.tensor_tensor(out=o, in0=o, in1=maskD2, op=ALU.mult), reads=[bPT[pi], bConst], writes=[bPT[pi]])
                else:
                    for jj in range(2):
                        lo = 0 if hasprev[jj] else 1
                        S.op("act", lambda e, o=PTv[:, jj, lo:2, :], i_=STv[:, jj, lo:2, :]: e.activation(out=o, in_=i_, func=AF.Exp),
                             reads=[bPS[sb]], writes=[bPT[pi]])
                        S.op("pool" if half == 0 else "dve", lambda e, o=PTv[:, jj, lo:2, :], m=md[:, jj, lo:2, :]: e.tensor_tensor(out=o, in0=o, in1=m, op=ALU.mult),
                             reads=[bPT[pi], bConst], writes=[bPT[pi]])
                info.append((pi, PTv, hasprev))
            ginfo[gidx] = info

        def dil_tail(gidx):
            nonlocal wi
            s, h, hb, d, blocks_, accv, last = groups[gidx]
            gp = gidx % 2
            ob = 6 + gp
            OT = bank(ob).rearrange("p (j i) -> p j i", j=4)
            info = ginfo.pop(gidx)
            for half in range(2):
                pi, PTv, hasprev = info[half]
                for jj in range(2):
                    r, n = blocks_[half * 2 + jj]
                    slot = half * 2 + jj
                    lastslot = (slot == 3)
                    S.op("pe", lambda e, o=OT[0:65, slot, :], l=Vv[d][:, blk(d, r, n), h, 0:65], r_=PTv[:, jj, 1, :], hp=hasprev[jj]:
                         e.matmul(o, lhsT=l, rhs=r_, start=True, stop=(not hp)), reads=[bVd[d], bPT[pi]], writes=[bPS[ob]],
                         inc=(lastslot and not hasprev[jj]))
                    if hasprev[jj]:
                        S.op("pe", lambda e, o=OT[0:65, slot, :], l=Vv[d][:, blk(d, r, n - 1), h, 0:65], r_=PTv[:, jj, 0, :]:
                             e.matmul(o, lhsT=l, rhs=r_, start=False, stop=True), reads=[bVd[d], bPT[pi]], writes=[bPS[ob]],
                             inc=lastslot)
            if d == 1:
                S.op("dve", lambda e, o=accv, i_=OT[0:65, :, :]: e.tensor_copy(out=o, in_=i_), reads=[bPS[ob]], writes=[bacc[hb]])
            else:
                S.op("dve", lambda e, o=accv, i_=OT[0:65, :, :]: e.tensor_tensor(out=o, in0=i_, in1=o, op=ALU.add),
                     reads=[bPS[ob], bacc[hb]], writes=[bacc[hb]])
            if last:
                ac = acc[hb]
                for qc in range(8):
                    fa, fb = finalize_ops(ac[:, qc * 512:(qc + 1) * 512], bacc[hb], NH + h, s, qc, work, wi, (4, 5))
                    wi += 1
                    pending.append(fa)
                    pending.append(fb)
            elif pending:
                pending.popleft()()

        ng = len(groups)
        done_tail = -1
        for g in range(ng + 1):
            if g < ng:
                if g >= 1 and groups[g][0] != groups[g - 1][0]:
                    dil_tail(g - 1)
                    done_tail = g - 1
                dil_head(g)
            if g >= 1 and done_tail != g - 1:
                dil_tail(g - 1)
        while pending:
            pending.popleft()()
        S.barrier()

        if stop == 'dil':
            S.emit()
            return nc
        def layer_norm(y, by, dst, bdst, gtile, btile, lnw, epsb, bLNc):
            stt, mv, rstd, nb, bst = lnw
            S.op("dve", lambda e: e.bn_stats(out=stt[:, 0:6], in_=y[:, 0:512]), reads=[by], writes=[bst])
            S.op("dve", lambda e: e.bn_stats(out=stt[:, 6:12], in_=y[:, 512:1024]), reads=[by], writes=[bst])
            S.op("dve", lambda e: e.bn_aggr(out=mv[:, 0:2], in_=stt[:, 0:12]), reads=[bst], writes=[bst])
            S.op("act", lambda e: e.activation(out=rstd, in_=mv[:, 1:2], func=AF.Sqrt, bias=epsb[:, 0:1], scale=1.0), reads=[bst, bConst], writes=[bst])
            S.op("dve", lambda e: e.reciprocal(out=rstd, in_=rstd), reads=[bst], writes=[bst])
            S.op("dve", lambda e: e.scalar_tensor_tensor(out=nb, in0=mv[:, 0:1], scalar=-1.0, in1=rstd, op0=ALU.mult, op1=ALU.mult), reads=[bst], writes=[bst])
            S.op("act", lambda e: e.activation(out=y, in_=y, func=AF.Identity, bias=nb[:, 0:1], scale=rstd[:, 0:1]), reads=[by, bst], writes=[by])
            S.op("pool", lambda e: e.tensor_tensor(out=y, in0=y, in1=gtile, op=ALU.mult), reads=[by, bLNc], writes=[by])
            S.op("pool", lambda e: e.tensor_tensor(out=dst, in0=y, in1=btile, op=ALU.add), reads=[by, bLNc], writes=[bdst])

        def alloc_lnw(n):
            return [(A.f32(12), A.f32(2), A.f32(1), A.f32(1), Buf()) for _ in range(n)]

        A.reset()
        A.limit = WUP_OFF
        wup = A.at_bf16(WUP_OFF, 8 * 2 * D_FF)
        wupv = wup.rearrange("p (c f) -> p c f", c=8)
        bWu = [Buf(), Buf()]
        wu_dv = wup_d.rearrange("(c p) f -> p c f", p=128)

        wup_jobs = []
        for halfsel in (0, 1):
            for c in range(8):
                for q4 in ((0, 2) if halfsel == 0 else (1, 3)):
                    wup_jobs.append((c, q4, halfsel))

        def load_wup_next():
            if wup_jobs:
                c, q4, halfsel = wup_jobs.pop(0)
                S.dma("pool", wupv[:, c, q4 * 1408:(q4 + 1) * 1408], wu_dv[:, c, q4 * 1408:(q4 + 1) * 1408], writes=[bWu[halfsel]])
        wout = A.bf16(8 * D)
        woutv = wout.rearrange("p (c f) -> p c f", c=8)
        bWo = Buf()
        wo_dv = wout_d.rearrange("(c p) f -> p c f", p=128)
        for c in range(8):
            S.dma("pool", woutv[:, c, :], wo_dv[:, c, :], writes=[bWo])
        bLNc1 = Buf()
        g1t = A.f32(1024)
        b1t = A.f32(1024)
        epsb1 = A.f32(1)
        S.op("dve", lambda e, t=epsb1: e.memset(t, LN_EPS), writes=[bConst])
        S.dma("sp", g1t, ln1g_d.broadcast_to([128, 1024]), writes=[bLNc1])
        S.dma("sp", b1t, ln1b_d.broadcast_to([128, 1024]), writes=[bLNc1])
        mtt = [A.bf16(8 * 512) for _ in range(2)]
        bmtt = [Buf(), Buf()]
        xt = [A.f32(1024) for _ in range(2)] * 2
        bxt = [Buf() for _ in range(2)] * 2
        tmpb = [A.f32(1024)] * 2
        btmp = [Buf()] * 2
        yb = [A.f32(1024) for _ in range(2)] * 2
        byb = [Buf() for _ in range(2)] * 2
        x1b = [A.f32(1024) for _ in range(3)]
        bx1 = [Buf() for _ in range(3)]
        lnw = alloc_lnw(2)
        h2st = [A.bf16(8 * 512) for _ in range(2)]
        bh2 = [Buf(), Buf()]
        zer = A.bf16(16)
        S.op("dve", lambda e, t=zer: e.memset(t, 0.0), writes=[bConst])
        tiles = [(s, J, i) for s in range(LIM["B_seq"]) for J in range(8) for i in range(4)]
        for s in range(LIM["B_seq"]):
            for c in range(8):
                S.dma("pool", H2T[s, c * 128:(c + 1) * 128, 0:2], zer[:, 0:2], reads=[bConst])

        def b1_load_mt(s, J):
            jb = (s * 8 + J) % 2
            mt_v = MT[s].rearrange("(c p) t -> p c t", p=128)
            mv3 = mtt[jb].rearrange("p (c t) -> p c t", c=8)
            for c in range(8):
                S.dma("sp", mv3[:, c, :], mt_v[:, c, J * 512:(J + 1) * 512], writes=[bmtt[jb]])

        def b1_head(t):
            s, J, i = tiles[t]
            if t >= 2 and (t % 2 == 0 or len(tiles) - t <= len(wup_jobs)):
                load_wup_next()
            jb = (s * 8 + J) % 2
            if i == 0:
                if t == 0:
                    b1_load_mt(s, J)
                if t + 4 < len(tiles):
                    b1_load_mt(tiles[t + 4][0], tiles[t + 4][1])
            mv3 = mtt[jb].rearrange("p (c t) -> p c t", c=8)
            T = J * 4 + i
            tok = slice(i * 128, (i + 1) * 128)
            S.dma("sp", xt[t % 2], x_d[s, T * 128:(T + 1) * 128, :], writes=[bxt[t % 2]])
            mixb = (2 * (t % 3), 2 * (t % 3) + 1)
            for half in range(2):
                for c in range(8):
                    S.op("pe", lambda e, o=bank(mixb[half]), l=mv3[:, c, tok], r=woutv[:, c, half * 512:(half + 1) * 512], c=c:
                         e.matmul(o, lhsT=l, rhs=r, start=(c == 0), stop=(c == 7)), reads=[bmtt[jb], bWo], writes=[bPS[mixb[half]]], inc=(c == 7))

        def b1_tail(t):
            s, J, i = tiles[t]
            jb = (s * 8 + J) % 2
            T = J * 4 + i
            tok = slice(i * 128, (i + 1) * 128)
            k3 = t % 3
            mixb = (2 * (t % 3), 2 * (t % 3) + 1)
            mix = PS[:, mixb[0] * 512:(mixb[0] + 2) * 512]
            tmp = tmpb[t % 2]
            S.op("dve", lambda e, i0=mix, g=gab[s], tmp=tmp: e.tensor_tensor(out=tmp, in0=i0, in1=g, op=ALU.mult),
                 reads=[bPS[mixb[0]], bPS[mixb[1]], bG], writes=[btmp[t % 2]])
            S.op("dve", lambda e, o=yb[t % 2], xx=xt[t % 2], tmp=tmp: e.scalar_tensor_tensor(out=o, in0=xx, scalar=ALPHA, in1=tmp, op0=ALU.mult, op1=ALU.add),
                 reads=[bxt[t % 2], btmp[t % 2]], writes=[byb[t % 2]])
            layer_norm(yb[t % 2], byb[t % 2], x1b[k3], bx1[k3], g1t, b1t, lnw[t % 2], epsb1, bLNc1)
            S.dma("pool", X1[s, T * 128:(T + 1) * 128, :], x1b[k3], reads=[bx1[k3]])

        def b1_tail2(t):
            s, J, i = tiles[t]
            jb = (s * 8 + J) % 2
            tok = slice(i * 128, (i + 1) * 128)
            k3 = t % 3
            h2v = h2st[jb].rearrange("p (c t) -> p c t", c=8)
            for c in range(8):
                pbk = 6 + c // 4
                S.op("pe", lambda e, o=bank(pbk)[:, (c % 4) * 128:(c % 4 + 1) * 128], i_=x1b[k3][:, c * 128:(c + 1) * 128]:
                     e.transpose(o, i_, ident32), reads=[bx1[k3], bConst], writes=[bPS[pbk]])
            for c in range(8):
                pbk = 6 + c // 4
                S.op("act", lambda e, o=h2v[:, c, tok], i_=bank(pbk)[:, (c % 4) * 128:(c % 4 + 1) * 128],
                     sca=sc1f[:, s * 8 + c:s * 8 + c + 1], bi=shf[:, s * 8 + c:s * 8 + c + 1]:
                     e.activation(out=o, in_=i_, func=AF.Identity, bias=bi, scale=sca), reads=[bPS[pbk], bAda], writes=[bh2[jb]])
            if i == 3:
                h2_v = H2T[s].rearrange("(c p) t -> p c t", p=128)
                for c in range(8):
                    S.dma("pool", h2_v[:, c, 2 + J * 512:2 + (J + 1) * 512], h2v[:, c, :], reads=[bh2[jb]])

        nt_ = len(tiles)
        for t in range(nt_ + 2):
            if t < nt_:
                b1_head(t)
            if 1 <= t <= nt_:
                b1_tail(t - 1)
            if t >= 2:
                b1_tail2(t - 2)
        S.barrier()

        if stop == 'B1':
            S.emit()
            return nc
        A.reset()
        A.limit = WDN_OFF
        while wup_jobs:
            load_wup_next()
        wdn = A.at_bf16(WDN_OFF, NJ * D)
        wdnv = wdn.rearrange("p (j f) -> p j f", j=NJ)
        bWd = Buf()
        wd_dv = wdn_d.rearrange("(j p) f -> p j f", p=128)
        wdn_jobs = list(range(NJ))

        def load_wdn_next():
            if wdn_jobs:
                j = wdn_jobs.pop(0)
                S.dma("pool", wdnv[:, j, :], wd_dv[:, j, :], writes=[bWd])
        h2t = [A.bf16(8 * 512) for _ in range(2)]
        bh2t = [Buf(), Buf()]
        ya = [A.f32(512) for _ in range(2)]
        yg = [A.f32(512) for _ in range(2)]
        sg = [A.f32(512) for _ in range(2)]
        actst = [A.bf16(512) for _ in range(3)]
        bya = [Buf(), Buf()]
        byg = [Buf(), Buf()]
        bsg = [Buf(), Buf()]
        bact = [Buf() for _ in range(3)]
        cwv = cw.rearrange("p (i j) -> p i j", i=3)
        ftiles = [(s, J) for s in range(LIM["B_seq"]) for J in range(9)]
        items = [(ti, j) for ti in range(len(ftiles)) for j in range(NJ)]

        def b2a_load(ti):
            s, J = ftiles[ti]
            t0 = 510 * J
            N = min(510, S_LEN - t0) + 2
            h2_v = H2T[s].rearrange("(c p) t -> p c t", p=128)
            hv = h2t[ti % 2].rearrange("p (c t) -> p c t", c=8)
            for c in range(8):
                S.dma("sp", hv[:, c, 0:N], h2_v[:, c, t0:t0 + N], writes=[bh2t[ti % 2]])

        def b2a_head(i):
            ti, j = items[i]
            if i % 12 == 6 or len(items) - i <= len(wdn_jobs):
                load_wdn_next()
            s, J = ftiles[ti]
            if j == 0:
                if ti == 0:
                    b2a_load(0)
                if ti + 1 < len(ftiles):
                    b2a_load(ti + 1)
            N = min(510, S_LEN - 510 * J) + 2
            hv = h2t[ti % 2].rearrange("p (c t) -> p c t", c=8)
            ub = (2 * (i % 4), 2 * (i % 4) + 1)
            for which in range(2):
                col0 = which * D_FF + j * 128
                for c in range(8):
                    S.op("pe", lambda e, o=bank(ub[which])[:, 0:N], l=wupv[:, c, col0:col0 + 128], r=hv[:, c, 0:N], c=c:
                         e.matmul(o, lhsT=l, rhs=r, start=(c == 0), stop=(c == 7)), reads=[bWu[0 if j < 11 else 1], bh2t[ti % 2]], writes=[bPS[ub[which]]], inc=(c == 7))

        def b2a_tail(i):
            ti, j = items[i]
            s, J = ftiles[ti]
            t0 = 510 * J
            ntok = min(510, S_LEN - t0)
            N = ntok + 2
            k = i % 2
            k3 = i % 3
            ub = (2 * (i % 4), 2 * (i % 4) + 1)
            for which, ydst, by_ in ((0, ya[k], bya[k]), (1, yg[k], byg[k])):
                u = bank(ub[which])
                jj = which * NJ + j
                S.op("act", lambda e, o=ydst[:, 0:ntok], i_=u[:, 2:N], sca=cwv[:, 2, jj:jj + 1], bi=cb[:, jj:jj + 1]:
                     e.activation(out=o, in_=i_, func=AF.Identity, bias=bi, scale=sca), reads=[bPS[ub[which]], bConst], writes=[by_])
                S.op("dve", lambda e, o=ydst[:, 0:ntok], i_=u[:, 1:N - 1], sca=cwv[:, 1, jj:jj + 1]:
                     e.scalar_tensor_tensor(out=o, in0=i_, scalar=sca, in1=o, op0=ALU.mult, op1=ALU.add), reads=[bPS[ub[which]], by_, bConst], writes=[by_])
                S.op("dve", lambda e, o=ydst[:, 0:ntok], i_=u[:, 0:N - 2], sca=cwv[:, 0, jj:jj + 1]:
                     e.scalar_tensor_tensor(out=o, in0=i_, scalar=sca, in1=o, op0=ALU.mult, op1=ALU.add), reads=[bPS[ub[which]], by_, bConst], writes=[by_])
            S.op("act", lambda e, o=sg[k][:, 0:ntok], i_=yg[k][:, 0:ntok]: e.activation(out=o, in_=i_, func=AF.Silu), reads=[byg[k]], writes=[bsg[k]])
            S.op("pool", lambda e, o=actst[k3][:, 0:ntok], a=sg[k][:, 0:ntok], b=ya[k][:, 0:ntok]: e.tensor_tensor(out=o, in0=a, in1=b, op=ALU.mult),
                 reads=[bsg[k], bya[k]], writes=[bact[k3]])
            S.dma("pool", ACTT[s, j * 128:(j + 1) * 128, t0:t0 + ntok], actst[k3][:, 0:ntok], reads=[bact[k3]])

        LA = 2
        for i in range(len(items) + LA):
            if i < len(items):
                b2a_head(i)
            if i >= LA:
                b2a_tail(i - LA)
        S.barrier()

        if stop == 'B2a':
            S.emit()
            return nc
        while wdn_jobs:
            load_wdn_next()
        A.reset()
        A.limit = WDN_OFF
        bLNc2 = Buf()
        g2t = A.f32(1024)
        b2t = A.f32(1024)
        S.dma("sp", g2t, ln2g_d.broadcast_to([128, 1024]), writes=[bLNc2])
        S.dma("sp", b2t, ln2b_d.broadcast_to([128, 1024]), writes=[bLNc2])
        epsb2 = A.f32(1)
        S.op("dve", lambda e, t=epsb2: e.memset(t, LN_EPS), writes=[bConst])
        low_top = A.top
        A.top = WUP_OFF
        A.limit = NW
        att = [A.bf16(NJ * 512) for _ in range(2)]
        batt = [Buf(), Buf()]
        x1t = [A.f32(1024) for _ in range(3)]
        bx1t = [Buf() for _ in range(3)]
        yb = [A.f32(1024) for _ in range(3)]
        byb = [Buf() for _ in range(3)]
        ob_ = [A.f32(1024) for _ in range(3)]
        bob = [Buf() for _ in range(3)]
        A.top = low_top
        A.limit = WDN_OFF
        tmpb = [A.f32(1024) for _ in range(2)]
        btmp = [Buf(), Buf()]
        lnw = alloc_lnw(2)
        tiles = [(s, J, i) for s in range(LIM["B_seq"]) for J in range(8) for i in range(4)]

        def b2b_load(s, J):
            jb = (s * 8 + J) % 2
            at_v = ACTT[s].rearrange("(j p) t -> p j t", p=128)
            av = att[jb].rearrange("p (j t) -> p j t", j=NJ)
            for j in range(NJ):
                S.dma("sp", av[:, j, :], at_v[:, j, J * 512:(J + 1) * 512], writes=[batt[jb]])

        def b2b_head(t):
            s, J, i = tiles[t]
            jb = (s * 8 + J) % 2
            if i == 0:
                if t == 0:
                    b2b_load(s, J)
                if t + 4 < len(tiles):
                    b2b_load(tiles[t + 4][0], tiles[t + 4][1])
            av = att[jb].rearrange("p (j t) -> p j t", j=NJ)
            T = J * 4 + i
            tok = slice(i * 128, (i + 1) * 128)
            S.dma("sp", x1t[t % 3], X1[s, T * 128:(T + 1) * 128, :], writes=[bx1t[t % 3]])
            fb = (2 * (t % 4), 2 * (t % 4) + 1)
            for half in range(2):
                for j in range(NJ):
                    S.op("pe", lambda e, o=bank(fb[half]), l=av[:, j, tok], r=wdnv[:, j, half * 512:(half + 1) * 512], j=j:
                         e.matmul(o, lhsT=l, rhs=r, start=(j == 0), stop=(j == NJ - 1)), reads=[batt[jb], bWd], writes=[bPS[fb[half]]], inc=(j == NJ - 1))

        def b2b_tail(t):
            s, J, i = tiles[t]
            T = J * 4 + i
            k3 = t % 3
            fb = (2 * (t % 4), 2 * (t % 4) + 1)
            ffn = PS[:, fb[0] * 512:(fb[0] + 2) * 512]
            tmp = tmpb[t % 2]
            S.op("dve", lambda e, i0=ffn, g=gfb[s], tmp=tmp: e.tensor_tensor(out=tmp, in0=i0, in1=g, op=ALU.mult),
                 reads=[bPS[fb[0]], bPS[fb[1]], bG], writes=[btmp[t % 2]])
            S.op("dve", lambda e, o=yb[k3], xx=x1t[k3], tmp=tmp: e.scalar_tensor_tensor(out=o, in0=xx, scalar=ALPHA, in1=tmp, op0=ALU.mult, op1=ALU.add),
                 reads=[bx1t[k3], btmp[t % 2]], writes=[byb[k3]])
            layer_norm(yb[k3], byb[k3], ob_[k3], bob[k3], g2t, b2t, lnw[t % 2], epsb2, bLNc2)
            S.dma("pool", out_d[s, T * 128:(T + 1) * 128, :], ob_[k3], reads=[bob[k3]])

        LA = 2
        for t in range(len(tiles) + LA):
            if t < len(tiles):
                b2b_head(t)
            if t >= LA:
                b2b_tail(t - LA)
        S.barrier()
        S.emit()
    return nc


_PROG = None


def _prep_inputs(core, x, c, positions, w_ada, b_ada, w_in, b_fgate, gn_a, gn_b, w_out,
                 ln1_g, ln1_b, w_up, conv_w, conv_b, w_down, ln2_g, ln2_b, consts):
    b0 = core * NSEQ
    xs = np.ascontiguousarray(x[b0:b0 + NSEQ])
    m = {
        "x": xs,
        "xT": np.ascontiguousarray(xs.transpose(0, 2, 1)),
        "c_arr": np.ascontiguousarray(c[b0:b0 + NSEQ].T.reshape(8, 128, NSEQ).transpose(1, 0, 2).reshape(128, 16)),
        "pos_arr": np.ascontiguousarray(positions[b0:b0 + NSEQ].reshape(NSEQ, 32, 128).transpose(0, 2, 1)),
    }
    m.update(consts)
    return m


def kernel(x, c, positions, w_ada, b_ada, w_in, b_fgate, gn_a, gn_b, w_out,
           ln1_g, ln1_b, w_up, conv_w, conv_b, w_down, ln2_g, ln2_b):
    global _PROG
    f32 = np.float32
    x = np.asarray(x, f32)
    c = np.asarray(c, f32)
    positions = np.asarray(positions, np.int32)
    a = lambda t: np.ascontiguousarray(np.asarray(t, f32))
    kq = np.arange(128)
    tri = (kq[None, :] >= kq[:, None]).astype(f32)
    freq = (500000.0 ** (-np.arange(0, 16, 2, dtype=np.float32) / np.float32(16))).astype(f32)
    consts = {
        "w_ada": a(w_ada[0]),
        "b_ada_fm": a(np.asarray(b_ada[0], f32).reshape(48, 128).T),
        "b_ada_row": a(np.asarray(b_ada[0], f32).reshape(1, 6 * D)),
        "w_in": a(w_in[0]),
        "b_fgate": a(np.asarray(b_fgate[0], f32).reshape(1, 8)),
        "gn": a(np.concatenate([np.asarray(gn_a[0], f32).reshape(8, 64).T, np.asarray(gn_b[0], f32).reshape(8, 64).T], axis=1)),
        "w_out": a(w_out[0]),
        "ln1_g": a(np.asarray(ln1_g[0], f32).reshape(1, D)),
        "ln1_b": a(np.asarray(ln1_b[0], f32).reshape(1, D)),
        "w_up": a(w_up[0]),
        "conv_wf": a(np.asarray(conv_w[0], f32).reshape(3, 44, 128).transpose(2, 0, 1).reshape(128, 132)),
        "conv_bf": a(np.asarray(conv_b[0], f32).reshape(44, 128).T),
        "w_down": a(w_down[0]),
        "ln2_g": a(np.asarray(ln2_g[0], f32).reshape(1, D)),
        "ln2_b": a(np.asarray(ln2_b[0], f32).reshape(1, D)),
        "ident": np.eye(128, dtype=f32),
        "tri": tri,
        "triT": np.ascontiguousarray(tri.T),
        "freq": np.ascontiguousarray(np.broadcast_to(freq[None, :], (128, 8))),
    }
    if _PROG is None:
        _PROG = build_program()
    in_maps = [_prep_inputs(core, x, c, positions, None, None, None, None, None, None, None,
                            None, None, None, None, None, None, None, None, consts) for core in range(8)]
    res = run_bass_kernel_spmd(_PROG, in_maps, core_ids=list(range(8)))
    out = np.concatenate([np.asarray(r["out"], f32) for r in res.results], axis=0)
    kernel.last_results = res.results
    return out
```

```python
import math
from contextlib import ExitStack
import numpy as np
import ml_dtypes
import concourse.bass as bass
import concourse.mybir as mybir
from concourse.bass_utils import run_bass_kernel_spmd

F32 = mybir.dt.float32
BF16 = mybir.dt.bfloat16
I32 = mybir.dt.int32
AF = mybir.ActivationFunctionType
ALU = mybir.AluOpType

NSEQ = 2
S_LEN = 4096
D = 1024
NH = 8
DH = 64
D_IN = 3080
D_FF = 2816
NJ = 22
ALPHA = 2.0 ** 0.25
LN_EPS = 1e-5
RMS_EPS = 1e-6
TWO_PI = 2.0 * math.pi
PI_SAFE = 3.1415925
C_QF, C_KF, C_VF, C_FA, C_QD, C_KD, C_VD = 0, 512, 1024, 1536, 1544, 2056, 2568
VW = 66
VROW = 8 * VW

SAME_ENGINE_SYNC = True
DEBUG = False
LIM = {"P_seq": NSEQ, "P_J": 8, "P_F": True, "P_parts": "qkdvf", "P_rope": True, "d_mode": 0, "A_seq": NSEQ, "A_heads": NH, "B_seq": NSEQ}


class Buf:
    __slots__ = ("w", "rs")

    def __init__(self):
        self.w = None
        self.rs = []


class Sched:
    ENG = ("pe", "act", "dve", "pool", "sp")

    def __init__(self, nc, stack):
        self.nc = nc
        self.ops = {e: [] for e in self.ENG}
        self.cnt = {}
        self.sem = {}
        for e in self.ENG:
            self.sem[e] = stack.enter_context(nc.semaphore("s_" + e))
            self.cnt[e] = 0
        self.dq = {}
        for q, n in (("sp", 16), ("pool", 16), ("act", 4)):
            keys = []
            for i in range(n):
                k = "d_%s%d" % (q, i)
                self.sem[k] = stack.enter_context(nc.semaphore("s_" + k))
                self.cnt[k] = 0
                keys.append(k)
            self.dq[q] = [keys, 0]
        self.seen = {e: {} for e in self.ENG}
        self.snap = {}

    def _collect(self, e, reads, writes, extra=()):
        need = {}

        def add(t):
            if t is None:
                return
            k, v = t
            if need.get(k, 0) < v:
                need[k] = v
        for b in reads:
            add(b.w)
        for b in writes:
            add(b.w)
            for t in b.rs:
                add(t)
        for t in extra:
            add(t)
        waits = []
        seen = self.seen[e]
        for k, v in need.items():
            if k == e and (e in ("pe", "sp") or not SAME_ENGINE_SYNC):
                continue
            if seen.get(k, 0) >= v:
                continue
            waits.append((k, v))
        for k, v in waits:
            sn = self.snap.get((k, v))
            if sn:
                for kk, vv in sn.items():
                    if seen.get(kk, 0) < vv:
                        seen[kk] = vv
            seen[k] = v
        return waits

    def _commit(self, ticket, e, reads, writes):
        self.snap[ticket] = dict(self.seen[e])
        for b in reads:
            b.rs.append(ticket)
            if len(b.rs) > 64:
                b.rs = b.rs[-48:]
        for b in writes:
            b.w = ticket
            b.rs = []

    def op(self, e, fn, reads=(), writes=(), inc=True):
        waits = self._collect(e, reads, writes)
        if inc:
            self.cnt[e] += 1
            ticket = (e, self.cnt[e])
            self.ops[e].append((waits, fn, (e, 1)))
        else:
            ticket = (e, self.cnt[e] + 1)
            self.ops[e].append((waits, fn, None))
        self._commit(ticket, e, reads, writes)
        return ticket

    def dma(self, q, out, in_, reads=(), writes=(), **kw):
        keys, rr = self.dq[q]
        k = keys[rr % len(keys)]
        self.dq[q][1] = rr + 1
        prev = (k, self.cnt[k]) if self.cnt[k] else None
        waits = self._collect(q, reads, writes, extra=(prev,) if prev else ())
        self.cnt[k] += 16
        ticket = (k, self.cnt[k])

        def fn(eng, out=out, in_=in_, kw=kw):
            return eng.dma_start(out=out, in_=in_, **kw)
        self.ops[q].append((waits, fn, (k, 16)))
        self._commit(ticket, q, reads, writes)
        return ticket

    def barrier(self):
        tickets = [(k, v) for k, v in self.cnt.items() if v > 0]
        for e in self.ENG:
            waits = self._collect(e, (), (), extra=tickets)
            self.ops[e].append((waits, None, None))

    def check(self):
        val = {k: 0 for k in self.sem}
        pc = {e: 0 for e in self.ENG}
        progress = True
        while progress:
            progress = False
            for e in self.ENG:
                ops = self.ops[e]
                while pc[e] < len(ops):
                    waits, fn, inc = ops[pc[e]]
                    if any(val[k] < v for k, v in waits):
                        break
                    if inc is not None:
                        val[inc[0]] += inc[1]
                    pc[e] += 1
                    progress = True
        stuck = {e: (pc[e], len(self.ops[e])) for e in self.ENG if pc[e] < len(self.ops[e])}
        for e, (p, n) in stuck.items():
            waits = self.ops[e][p][0]
            print("STUCK", e, p, n, [(k, v, val[k]) for k, v in waits if val[k] < v])
        print("check: ops per engine", {e: len(self.ops[e]) for e in self.ENG}, "stuck:", bool(stuck))
        return not stuck

    def emit(self):
        nc = self.nc
        if not self.check():
            raise RuntimeError("scheduler deadlock")
        with nc.Block() as block:
            def make(e):
                def body(eng):
                    for waits, fn, inc in self.ops[e]:
                        for k, v in waits:
                            eng.wait_ge(self.sem[k], v)
                        if fn is None:
                            continue
                        ins = fn(eng)
                        if inc is not None:
                            ins.then_inc(self.sem[inc[0]], inc[1])
                return body
            block.tensor(make("pe"))
            block.scalar(make("act"))
            block.vector(make("dve"))
            block.gpsimd(make("pool"))
            block.sync(make("sp"))


class Arena:
    def __init__(self, ap, nwords):
        self.ap = ap
        self.n = nwords
        self.top = 0
        self.base = 0

    def words(self, n):
        n = (n + 15) // 16 * 16
        off = self.top
        self.top += n
        assert self.top <= getattr(self, "limit", self.n), "SBUF arena overflow %d > %d" % (self.top, getattr(self, "limit", self.n))
        return off

    def f32(self, cols):
        off = self.words(cols)
        return self.ap[:, off:off + cols]

    def bf16(self, cols):
        nw = (cols + 1) // 2
        off = self.words(nw)
        return self.ap[:, off:off + nw].bitcast(BF16)[:, 0:cols]

    def i32(self, cols):
        off = self.words(cols)
        return self.ap[:, off:off + cols].bitcast(I32)

    def at_bf16(self, off_words, cols):
        nw = (cols + 1) // 2
        assert off_words + nw <= self.n
        return self.ap[:, off_words:off_words + nw].bitcast(BF16)[:, 0:cols]

    def mark(self):
        self.base = self.top

    def reset(self):
        self.top = self.base


def build_program(stop=None, debug=False):
    global DEBUG
    DEBUG = debug
    nc = bass.Bass("TRN2", target_bir_lowering=False)

    def din(name, shape, dt=F32):
        return nc.dram_tensor(name, list(shape), dt, kind="ExternalInput").ap()

    def dscr(name, shape, dt):
        return nc.dram_tensor(name, list(shape), dt, kind="ExternalOutput" if DEBUG else "Internal").ap()

    xT_d = din("xT", [NSEQ, D, S_LEN])
    x_d = din("x", [NSEQ, S_LEN, D])
    c_d = din("c_arr", [128, 16])
    pos_d = din("pos_arr", [NSEQ, 128, 32], I32)
    wada_d = din("w_ada", [D, 6 * D])
    bada_fm_d = din("b_ada_fm", [128, 48])
    bada_row_d = din("b_ada_row", [1, 6 * D])
    win_d = din("w_in", [D, D_IN])
    bfg_d = din("b_fgate", [1, 8])
    gn_d = din("gn", [64, 16])
    wout_d = din("w_out", [D, D])
    ln1g_d = din("ln1_g", [1, D])
    ln1b_d = din("ln1_b", [1, D])
    wup_d = din("w_up", [D, 2 * D_FF])
    cw_d = din("conv_wf", [128, 3 * 44])
    cb_d = din("conv_bf", [128, 44])
    wdn_d = din("w_down", [D_FF, D])
    ln2g_d = din("ln2_g", [1, D])
    ln2b_d = din("ln2_b", [1, D])
    ident_d = din("ident", [128, 128])
    tri_d = din("tri", [128, 128])
    triT_d = din("triT", [128, 128])
    freq_d = din("freq", [128, 8])
    out_d = nc.dram_tensor("out", [NSEQ, S_LEN, D], F32, kind="ExternalOutput").ap()

    QTF = dscr("QTF", [NSEQ, 512, S_LEN], BF16)
    KTF = dscr("KTF", [NSEQ, 512, S_LEN], BF16)
    QTD = dscr("QTD", [NSEQ, 512, S_LEN], BF16)
    KTD = dscr("KTD", [NSEQ, 512, S_LEN], BF16)
    VF = dscr("VF", [NSEQ, S_LEN, VROW], BF16)
    VD = dscr("VD", [NSEQ, S_LEN, VROW], BF16)
    FR = dscr("FR", [NSEQ, 24, S_LEN], BF16)
    MT = dscr("MT", [NSEQ, D, S_LEN], BF16)
    X1 = dscr("X1", [NSEQ, S_LEN, D], F32)
    H2T = dscr("H2T", [NSEQ, D, S_LEN + 2], BF16)
    ACTT = dscr("ACTT", [NSEQ, D_FF, S_LEN], BF16)

    with ExitStack() as st:
        S = Sched(nc, st)
        NW = 52000
        arena_t = st.enter_context(nc.sbuf_tensor("arena", [128, NW], F32))
        A = Arena(arena_t, NW)
        PS = st.enter_context(nc.psum_tensor("ps", [128, 4096], F32))

        def bank(i):
            return PS[:, i * 512:(i + 1) * 512]
        bPS = [Buf() for _ in range(8)]

        ident32 = A.f32(128)
        identb = A.bf16(128)
        trib = A.bf16(128)
        maskD2 = A.bf16(512)
        tri32 = A.f32(128)
        maskneg = A.bf16(128)
        ones32 = A.f32(128)
        freq = A.f32(8)
        gain = A.f32(16)
        bfg = A.f32(8)
        cw = A.f32(132)
        cb = A.f32(44)
        onesE = A.bf16(64)
        adaT = A.f32(96)
        sc1a = A.f32(16)
        sha = A.f32(16)
        sc1f = A.f32(16)
        shf = A.f32(16)
        gab = [A.f32(1024) for _ in range(NSEQ)]
        gfb = [A.f32(1024) for _ in range(NSEQ)]
        negF = [A.f32(256) for _ in range(NSEQ)]
        bConst = Buf()
        bAda = Buf()
        bNegF = [Buf() for _ in range(NSEQ)]
        bG = Buf()

        S.dma("sp", ident32, ident_d, writes=[bConst])
        S.dma("sp", tri32, tri_d, writes=[bConst])
        S.dma("sp", freq, freq_d, writes=[bConst])
        S.dma("sp", gain[0:64, :], gn_d, writes=[bConst])
        S.dma("sp", bfg, bfg_d.broadcast_to([128, 8]), writes=[bConst])
        S.dma("sp", cw, cw_d, writes=[bConst])
        S.dma("sp", cb, cb_d, writes=[bConst])
        S.dma("pool", identb, ident_d, writes=[bConst])
        S.dma("pool", trib, tri_d, writes=[bConst])
        md = maskD2.rearrange("p (j h q) -> p j h q", j=2, h=2)
        for j in range(2):
            S.dma("pool", md[:, j, 0, :], triT_d, writes=[bConst])
            S.dma("pool", md[:, j, 1, :], tri_d, writes=[bConst])
        S.op("dve", lambda e: e.memset(ones32, 1.0), writes=[bConst])
        S.op("dve", lambda e: e.tensor_scalar(out=maskneg, in0=tri32, scalar1=-1.0, scalar2=30000.0, op0=ALU.add, op1=ALU.mult), reads=[bConst], writes=[bConst])
        S.op("dve", lambda e: e.memset(onesE[0:64, :], 1.0), writes=[bConst])
        S.op("dve", lambda e: e.memset(onesE[64:65, :], 64.0 * RMS_EPS), writes=[bConst])
        A.mark()

        A.reset()
        WIN_OFF = NW - 12320
        WUP_OFF = NW - 22528
        WDN_OFF = WUP_OFF - 11264
        A.limit = WIN_OFF
        win = A.at_bf16(WIN_OFF, 8 * D_IN)
        winv = win.rearrange("p (c f) -> p c f", c=8)
        bWin = Buf()
        win_dv = win_d.rearrange("(c p) f -> p c f", p=128)
        for c in range(8):
            for hf in range(2):
                S.dma("pool", winv[:, c, hf * 1540:(hf + 1) * 1540], win_dv[:, c, hf * 1540:(hf + 1) * 1540], writes=[bWin])
        c_sb = A.f32(16)
        sc = A.f32(16)
        screp = [A.f32(1024) for _ in range(NSEQ)]
        badafm = A.f32(48)
        badarow = A.f32(2048)
        wg = [A.f32(8192) for _ in range(2)]
        bwg = [Buf(), Buf()]
        bsc = Buf()
        S.dma("sp", c_sb, c_d, writes=[bsc])
        S.dma("sp", badafm, bada_fm_d, writes=[bsc])
        S.dma("sp", badarow[:, 0:1024], bada_row_d[:, 2048:3072].broadcast_to([128, 1024]), writes=[bsc])
        S.dma("sp", badarow[:, 1024:2048], bada_row_d[:, 5120:6144].broadcast_to([128, 1024]), writes=[bsc])
        S.op("act", lambda e: e.activation(out=sc, in_=c_sb, func=AF.Silu), reads=[bsc], writes=[bsc])
        scv = sc.rearrange("p (c b) -> p c b", b=2)
        for s in range(NSEQ):
            rv = screp[s].rearrange("p (c m) -> p c m", m=128)
            for c in range(8):
                S.op("dve", lambda e, o=rv[:, c, :], sca=scv[:, c, s:s + 1]: e.tensor_scalar(
                    out=o, in0=ones32, scalar1=sca, scalar2=None, op0=ALU.mult), reads=[bsc, bConst], writes=[bsc])
        wada_v = wada_d.rearrange("(c p) f -> p c f", p=128)
        adaps = bank(7)[:, 0:96]
        for gi in range(6):
            wt = wg[gi % 2]
            wv = wt.rearrange("p (c f) -> p c f", c=8)
            for c in range(8):
                S.dma("sp", wv[:, c, :], wada_v[:, c, gi * 1024:(gi + 1) * 1024], writes=[bwg[gi % 2]])
            for fc in range(8):
                col = (gi * 8 + fc) * 2
                for c in range(8):
                    S.op("pe", lambda e, o=adaps[:, col:col + 2], l=wv[:, c, fc * 128:(fc + 1) * 128], r=scv[:, c, :], c=c:
                         e.matmul(o, lhsT=l, rhs=r, start=(c == 0), stop=(c == 7)),
                         reads=[bwg[gi % 2], bsc], writes=[bPS[7]], inc=(c == 7))
            if gi in (2, 5):
                dst = gab if gi == 2 else gfb
                brow = badarow[:, 0:1024] if gi == 2 else badarow[:, 1024:2048]
                for s in range(NSEQ):
                    rv = screp[s].rearrange("p (c m) -> p c m", m=128)
                    for half in range(2):
                        pb = bank(half)
                        for c in range(8):
                            S.op("pe", lambda e, o=pb, l=rv[:, c, :], r=wv[:, c, half * 512:(half + 1) * 512], c=c:
                                 e.matmul(o, lhsT=l, rhs=r, start=(c == 0), stop=(c == 7)),
                                 reads=[bwg[gi % 2], bsc], writes=[bPS[half]], inc=(c == 7))
                        S.op("dve", lambda e, o=dst[s][:, half * 512:(half + 1) * 512], i0=pb, i1=brow[:, half * 512:(half + 1) * 512]:
                             e.tensor_tensor(out=o, in0=i0, in1=i1, op=ALU.add), reads=[bPS[half], bsc], writes=[bG])
        adaTv = adaT.rearrange("p (k b) -> p k b", b=2)
        adapv = adaps.rearrange("p (k b) -> p k b", b=2)
        for b in range(2):
            S.op("dve", lambda e, o=adaTv[:, :, b], i0=adapv[:, :, b]: e.tensor_tensor(out=o, in0=i0, in1=badafm, op=ALU.add),
                 reads=[bPS[7], bsc], writes=[bAda])
        for s in range(NSEQ):
            S.op("dve", lambda e, o=sc1a[:, s * 8:(s + 1) * 8], i=adaTv[:, 8:16, s]: e.tensor_scalar(
                out=o, in0=i, scalar1=1.0, scalar2=None, op0=ALU.add), reads=[bAda], writes=[bAda])
            S.op("dve", lambda e, o=sha[:, s * 8:(s + 1) * 8], i=adaTv[:, 0:8, s]: e.tensor_copy(out=o, in_=i), reads=[bAda], writes=[bAda])
            S.op("dve", lambda e, o=sc1f[:, s * 8:(s + 1) * 8], i=adaTv[:, 32:40, s]: e.tensor_scalar(
                out=o, in0=i, scalar1=1.0, scalar2=None, op0=ALU.add), reads=[bAda], writes=[bAda])
            S.op("dve", lambda e, o=shf[:, s * 8:(s + 1) * 8], i=adaTv[:, 24:32, s]: e.tensor_copy(out=o, in_=i), reads=[bAda], writes=[bAda])
        S.barrier()

        if stop == 'ada':
            S.emit()
            return nc
        A.reset()
        A.limit = WIN_OFF
        xst = [A.f32(512) for _ in range(4)]
        bxst = [Buf() for _ in range(4)]
        hT = [A.bf16(8 * 512) for _ in range(2)]
        bhT = [Buf() for _ in range(2)]
        zb = [A.bf16(512) for _ in range(4)]
        bzb = [Buf() for _ in range(4)]
        tstage = {}
        for nm in ("QF", "KF", "QD", "KD"):
            tstage[nm] = ([A.bf16(4 * 512) for _ in range(2)], [Buf() for _ in range(2)])
        vst = [A.bf16(VROW) for _ in range(4)]
        bvst = [Buf() for _ in range(4)]
        rtmp = [A.f32(64) for _ in range(4)]
        z32 = [A.f32(512) for _ in range(2)]
        bz32 = [Buf(), Buf()]
        brtmp = [Buf() for _ in range(4)]
        posi = A.i32(32)
        posf = A.f32(32)
        ang = A.f32(256)
        kfl = A.f32(256)
        kin = A.i32(256)
        rr = A.f32(256)
        rc = A.f32(256)
        mm = A.f32(256)
        cosq = A.f32(256)
        sinq = A.f32(256)
        cosk = A.f32(256)
        sink = A.f32(256)
        tab8 = {}
        for nm_ in ("cosk", "sink"):
            tab8[nm_] = A.f32(32 * 64)
        LF = A.f32(256)
        fax = A.f32(8)
        faa = A.f32(8)
        carry = A.f32(256)
        Fm = A.f32(256)
        Fhi = A.bf16(256)
        Fr1 = A.f32(256)
        Fmid = A.bf16(256)
        Fr2 = A.f32(256)
        Flo = A.bf16(256)
        Fs = A.bf16(32 * 24)
        FsT = A.bf16(S_LEN)
        bRope = Buf()
        bLF = Buf()
        bF = Buf()
        for i in range(4):
            S.op("dve", lambda e, o=vst[i]: e.memset(o, 1.0), writes=[bvst[i]])

        LFv = LF.rearrange("p (n h) -> p n h", h=8)
        R = [bRope]

        def rope_tables(s):
            S.dma("sp", posi, pos_d[s], writes=[bRope])
            S.op("dve", lambda e: e.tensor_copy(out=posf, in_=posi), reads=[bRope], writes=[bRope])
            angv = ang.rearrange("p (n j) -> p n j", j=8)
            for n in range(32):
                S.op("dve", lambda e, o=angv[:, n, :], sca=posf[:, n:n + 1]: e.tensor_scalar(
                    out=o, in0=freq, scalar1=sca, scalar2=None, op0=ALU.mult), reads=[bRope, bConst], writes=[bRope])
            S.op("dve", lambda e: e.tensor_scalar(out=kfl, in0=ang, scalar1=1.0 / TWO_PI, scalar2=None, op0=ALU.mult), reads=R, writes=R)
            S.op("dve", lambda e: e.tensor_copy(out=kin, in_=kfl), reads=R, writes=R)
            S.op("dve", lambda e: e.tensor_copy(out=kfl, in_=kin), reads=R, writes=R)
            S.op("dve", lambda e: e.scalar_tensor_tensor(out=rr, in0=kfl, scalar=-6.28125, in1=ang, op0=ALU.mult, op1=ALU.add), reads=R, writes=R)
            S.op("dve", lambda e: e.scalar_tensor_tensor(out=rr, in0=kfl, scalar=-(TWO_PI - 6.28125), in1=rr, op0=ALU.mult, op1=ALU.add), reads=R, writes=R)
            S.op("dve", lambda e: e.tensor_scalar(out=rc, in0=rr, scalar1=math.pi / 2, scalar2=None, op0=ALU.add), reads=R, writes=R)
            S.op("dve", lambda e: e.tensor_scalar(out=mm, in0=rc, scalar1=math.pi, scalar2=None, op0=ALU.is_gt), reads=R, writes=R)
            S.op("dve", lambda e: e.scalar_tensor_tensor(out=rc, in0=mm, scalar=-TWO_PI, in1=rc, op0=ALU.mult, op1=ALU.add), reads=R, writes=R)
            for t in (rr, rc):
                S.op("dve", lambda e, t=t: e.tensor_scalar(out=t, in0=t, scalar1=-PI_SAFE, scalar2=PI_SAFE, op0=ALU.max, op1=ALU.min), reads=R, writes=R)
            S.op("act", lambda e: e.activation(out=sink, in_=rr, func=AF.Sin), reads=R, writes=R)
            S.op("act", lambda e: e.activation(out=cosk, in_=rc, func=AF.Sin), reads=R, writes=R)
            for nm_, t_ in (("cosk", cosk), ("sink", sink)):
                t8 = tab8[nm_].rearrange("p (n h j) -> p n h j", h=8, j=8)
                for hh in range(8):
                    S.op("dve", lambda e, o=t8[:, :, hh, :], i_=t_.rearrange("p (n j) -> p n j", j=8): e.tensor_copy(out=o, in_=i_), reads=R, writes=R)

        def f_stage(s):
            fps = bank(6)[:, 0:256]
            rps = bank(7)[:, 0:256]
            S.op("pe", lambda e: e.matmul(fps, lhsT=tri32, rhs=LF, start=True, stop=True), reads=[bLF, bConst], writes=[bPS[6]])
            S.op("pe", lambda e: e.matmul(rps, lhsT=ones32, rhs=LF, start=True, stop=True), reads=[bLF, bConst], writes=[bPS[7]])
            carv = carry.rearrange("p (n h) -> p n h", h=8)
            rpv = rps.rearrange("p (n h) -> p n h", h=8)
            S.op("dve", lambda e: e.memset(carv[:, 0, :], 0.0), writes=[bF])
            for n in range(1, 32):
                S.op("dve", lambda e, o=carv[:, n, :], a=rpv[:, n - 1, :], b=carv[:, n - 1, :]: e.tensor_tensor(out=o, in0=a, in1=b, op=ALU.add),
                     reads=[bPS[7], bF], writes=[bF])
            S.op("dve", lambda e: e.tensor_tensor(out=Fm, in0=fps, in1=carry, op=ALU.add), reads=[bPS[6], bF], writes=[bF])
            S.op("dve", lambda e, o=negF[s]: e.tensor_scalar(out=o, in0=Fm, scalar1=-1.0, scalar2=None, op0=ALU.mult), reads=[bF], writes=[bNegF[s]])
            S.op("dve", lambda e: e.tensor_copy(out=Fhi, in_=Fm), reads=[bF], writes=[bF])
            S.op("dve", lambda e: e.tensor_tensor(out=Fr1, in0=Fm, in1=Fhi, op=ALU.subtract), reads=[bF], writes=[bF])
            S.op("dve", lambda e: e.tensor_copy(out=Fmid, in_=Fr1), reads=[bF], writes=[bF])
            S.op("dve", lambda e: e.tensor_tensor(out=Fr2, in0=Fr1, in1=Fmid, op=ALU.subtract), reads=[bF], writes=[bF])
            S.op("dve", lambda e: e.tensor_copy(out=Flo, in_=Fr2), reads=[bF], writes=[bF])
            Fsv = Fs.rearrange("p (n j h) -> p n j h", j=3, h=8)
            for j, src_ in enumerate((Fhi, Fmid, Flo)):
                S.op("dve", lambda e, o=Fsv[:, :, j, :], i_=src_.rearrange("p (n h) -> p n h", h=8): e.tensor_copy(out=o, in_=i_), reads=[bF], writes=[bF])
            Fs2 = Fs.rearrange("p (n k) -> p n k", k=24)
            for g4 in range(4):
                tbi = 4 + (g4 % 2)
                tb = bank(tbi).bitcast(BF16).rearrange("p (g t) -> p g t", t=128)
                for k in range(8):
                    n = g4 * 8 + k
                    S.op("pe", lambda e, o=tb[0:24, k, :], i_=Fs2[:, n, :]: e.transpose(o, i_, identb), reads=[bF, bConst], writes=[bPS[tbi]])
                S.op("act", lambda e, o=FsT[0:24, g4 * 1024:(g4 + 1) * 1024], i_=bank(tbi).bitcast(BF16)[0:24, :]: e.activation(out=o, in_=i_, func=AF.Copy),
                     reads=[bPS[tbi]], writes=[bF])
            S.dma("pool", FR[s], FsT[0:24, :], reads=[bF])

        GRPS = [g_ for g_ in ("QF", "KF", "QD", "KD", "VF", "VD", "FA")
                if {"QF": "q", "KF": "q", "QD": "d", "KD": "d", "VF": "v", "VD": "v", "FA": "f"}[g_] in LIM["P_parts"]]
        GINFO = {"QF": (C_QF, 512, 0.125, QTF), "KF": (C_KF, 512, 1.0, KTF), "QD": (C_QD, 512, 0.125, QTD), "KD": (C_KD, 512, 1.0, KTD),
                 "VF": (C_VF, 512, 1.0, VF), "VD": (C_VD, 512, 1.0, VD), "FA": (C_FA, 8, 1.0, None)}
        jtiles = [(s, J) for s in range(LIM["P_seq"]) for J in range(LIM["P_J"])]
        pitems = [(ji, i, g_) for ji in range(len(jtiles)) for i in range(4) for g_ in GRPS]
        xcount = [0]

        def make_hT(ji):
            s, J = jtiles[ji]
            hb = ji % 2
            hTv = hT[hb].rearrange("p (c t) -> p c t", c=8)
            for c in range(8):
                xs = xcount[0] % 4
                xcount[0] += 1
                S.dma("sp", xst[xs], xT_d[s, c * 128:(c + 1) * 128, J * 512:(J + 1) * 512], writes=[bxst[xs]])
                S.op("act", lambda e, o=hTv[:, c, :], i=xst[xs], sca=sc1a[:, s * 8 + c:s * 8 + c + 1], bi=sha[:, s * 8 + c:s * 8 + c + 1]:
                     e.activation(out=o, in_=i, func=AF.Identity, bias=bi, scale=sca),
                     reads=[bxst[xs], bAda], writes=[bhT[hb]])

        def p_head(k):
            ji, i, g_ = pitems[k]
            s, J = jtiles[ji]
            if i == 0 and g_ == GRPS[0]:
                if ji == 0:
                    make_hT(0)
                if ji + 1 < len(jtiles):
                    make_hT(ji + 1)
            hb = ji % 2
            hTv = hT[hb].rearrange("p (c t) -> p c t", c=8)
            tok = slice(i * 128, (i + 1) * 128)
            col0, ncols, scl, dram = GINFO[g_]
            pbi = k % 4
            pb = bank(pbi)[:, 0:ncols]
            for c in range(8):
                S.op("pe", lambda e, o=pb, l=hTv[:, c, tok], r=winv[:, c, col0:col0 + ncols], c=c:
                     e.matmul(o, lhsT=l, rhs=r, start=(c == 0), stop=(c == 7)),
                     reads=[bhT[hb], bWin], writes=[bPS[pbi]], inc=(c == 7))

        rope_done = set()
        tcnt = [0]
        vcnt = [0]
        rcnt = [0]

        def p_transposes(k, nm, zbuf, bz, dram, s, J, i):
            stg, bst_ = tstage[nm]
            sb = (s * 8 + J) % 2
            tok = slice(i * 128, (i + 1) * 128)
            tbi = 4 + (tcnt[0] % 2)
            tcnt[0] += 1
            tb = bank(tbi).bitcast(BF16).rearrange("p (g t) -> p g t", t=128)
            for g in range(4):
                S.op("pe", lambda e, o=tb[:, g, :], i_=zbuf[:, g * 128:(g + 1) * 128]: e.transpose(o, i_, identb),
                     reads=[bz, bConst], writes=[bPS[tbi]])
            sv = stg[sb].rearrange("p (g t) -> p g t", g=4)
            S.op("act", lambda e, o=sv[:, :, tok], i_=tb[:, 0:4, :]: e.activation(out=o, in_=i_, func=AF.Copy),
                 reads=[bPS[tbi]], writes=[bst_[sb]])
            if i == 3:
                for g in range(4):
                    S.dma("pool", dram[s, g * 128:(g + 1) * 128, J * 512:(J + 1) * 512], sv[:, g, :], reads=[bst_[sb]])

        def p_tail(k):
            ji, i, g_ = pitems[k]
            s, J = jtiles[ji]
            n = J * 4 + i
            col0, ncols, scl, dram = GINFO[g_]
            pbi = k % 4
            pb = bank(pbi)[:, 0:ncols]
            zi = k % 4
            if g_ in ("QF", "KF"):
                S.op("act", lambda e, o=zb[zi], i_=pb, scl=scl: e.activation(out=o, in_=i_, func=AF.Copy, scale=scl),
                     reads=[bPS[pbi]], writes=[bzb[zi]])
                p_transposes(k, g_, zb[zi], bzb[zi], dram, s, J, i)
            elif g_ in ("QD", "KD"):
                if s not in rope_done:
                    rope_done.add(s)
                    rope_tables(s)
                ct, stb = tab8["cosk"], tab8["sink"]
                zv = zb[zi].rearrange("p (h d) -> p h d", d=64)
                z3i = rcnt[0] % 2
                rcnt[0] += 1
                z3 = z32[z3i]
                S.op("act", lambda e, o=z3, i_=pb, scl=scl: e.activation(out=o, in_=i_, func=AF.Copy, scale=scl),
                     reads=[bPS[pbi]], writes=[bz32[z3i]])
                pv = z3.rearrange("p (h d) -> p h d", d=64)
                cbv = ct.rearrange("p (n h j) -> p n h j", h=8, j=8)[:, n, :, :]
                sbv = stb.rearrange("p (n h j) -> p n h j", h=8, j=8)[:, n, :, :]
                t1 = pv[:, :, 0:8]
                t2 = pv[:, :, 8:16]
                S.op("act", lambda e, o=zv[:, :, 16:64], i_=pv[:, :, 16:64]: e.activation(out=o, in_=i_, func=AF.Copy),
                     reads=[bz32[z3i]], writes=[bzb[zi]])
                tA = rtmp[0].rearrange("p (h d) -> p h d", d=8)
                tB = rtmp[1].rearrange("p (h d) -> p h d", d=8)
                tC = rtmp[2].rearrange("p (h d) -> p h d", d=8)
                tD = rtmp[3].rearrange("p (h d) -> p h d", d=8)
                RD = [bz32[z3i], bRope]
                S.op("dve", lambda e, o=tA, a=t1, b=cbv: e.tensor_tensor(out=o, in0=a, in1=b, op=ALU.mult), reads=RD, writes=[brtmp[0]])
                S.op("dve", lambda e, o=tB, a=t2, b=sbv: e.tensor_tensor(out=o, in0=a, in1=b, op=ALU.mult), reads=RD, writes=[brtmp[1]])
                S.op("dve", lambda e, o=tC, a=t2, b=cbv: e.tensor_tensor(out=o, in0=a, in1=b, op=ALU.mult), reads=RD, writes=[brtmp[2]])
                S.op("dve", lambda e, o=tD, a=t1, b=sbv: e.tensor_tensor(out=o, in0=a, in1=b, op=ALU.mult), reads=RD, writes=[brtmp[3]])
                S.op("dve", lambda e, o=zv[:, :, 0:8], a=tA, b=tB: e.tensor_tensor(out=o, in0=a, in1=b, op=ALU.subtract),
                     reads=[brtmp[0], brtmp[1]], writes=[bzb[zi]])
                S.op("dve", lambda e, o=zv[:, :, 8:16], a=tC, b=tD: e.tensor_tensor(out=o, in0=a, in1=b, op=ALU.add),
                     reads=[brtmp[2], brtmp[3]], writes=[bzb[zi]])
                p_transposes(k, g_, zb[zi], bzb[zi], dram, s, J, i)
            elif g_ in ("VF", "VD"):
                vi = vcnt[0] % 4
                vcnt[0] += 1
                vv = vst[vi].rearrange("p (h e) -> p h e", e=VW)
                S.op("dve", lambda e, o=vv[:, :, 0:64], i_=pb.rearrange("p (h d) -> p h d", d=64): e.tensor_copy(out=o, in_=i_),
                     reads=[bPS[pbi]], writes=[bvst[vi]])
                S.dma("pool", dram[s, n * 128:(n + 1) * 128, :], vst[vi], reads=[bvst[vi]])
            else:
                S.op("dve", lambda e, i_=pb: e.tensor_tensor(out=fax, in0=i_, in1=bfg, op=ALU.add), reads=[bPS[pbi], bConst], writes=[bLF])
                S.op("dve", lambda e: e.scalar_tensor_tensor(out=faa, in0=fax, scalar=-1.0, in1=fax, op0=ALU.mult, op1=ALU.max), reads=[bLF], writes=[bLF])
                S.op("act", lambda e: e.activation(out=faa, in_=faa, func=AF.Exp, scale=-1.0), reads=[bLF], writes=[bLF])
                S.op("act", lambda e: e.activation(out=faa, in_=faa, func=AF.Ln, bias=1.0, scale=1.0), reads=[bLF], writes=[bLF])
                S.op("dve", lambda e, o=LFv[:, n, :]: e.scalar_tensor_tensor(out=o, in0=fax, scalar=0.0, in1=faa, op0=ALU.min, op1=ALU.subtract),
                     reads=[bLF], writes=[bLF])
            if LIM["P_F"] and (k + 1 == len(pitems) or jtiles[pitems[k + 1][0]][0] != s):
                f_stage(s)

        LA = 2
        for k in range(len(pitems) + LA):
            if k < len(pitems):
                p_head(k)
            if k >= LA:
                p_tail(k - LA)
        S.barrier()

        if stop == 'P':
            S.emit()
            return nc
        A.limit = NW
        from collections import deque

        def finalize_ops(src, bsrc, h_glob, s, qc, work, wi, ssbanks):
            sq, lnv, rinv, mg, bsq, bln, brv, bmg = work
            k = wi % 2
            ssi = ssbanks[wi % len(ssbanks)]
            ssb = bank(ssi)

            def fa():
                S.op("act", lambda e, o=sq[k][0:65, :], i_=src[0:65, :]: e.activation(out=o, in_=i_, func=AF.Square), reads=[bsrc], writes=[bsq[k]])
                S.op("pe", lambda e, o=ssb[0:64, :], r=sq[k][0:65, :]: e.matmul(o, lhsT=onesE[0:65, 0:64], rhs=r, start=True, stop=True),
                     reads=[bsq[k], bConst], writes=[bPS[ssi]])

            def fb():
                S.op("act", lambda e, o=lnv[k][0:64, :], i_=ssb[0:64, :]: e.activation(out=o, in_=i_, func=AF.Ln, scale=1.0 / 64.0), reads=[bPS[ssi]], writes=[bln[k]])
                S.op("act", lambda e, o=rinv[k][0:64, :], i_=lnv[k][0:64, :]: e.activation(out=o, in_=i_, func=AF.Exp, scale=-0.5), reads=[bln[k]], writes=[brv[k]])
                S.op("dve", lambda e, o=mg[k][0:64, :], a=src[0:64, :], g=gain[0:64, h_glob:h_glob + 1], b=rinv[k][0:64, :]:
                     e.scalar_tensor_tensor(out=o, in0=a, scalar=g, in1=b, op0=ALU.mult, op1=ALU.mult),
                     reads=[bsrc, brv[k], bConst], writes=[bmg[k]])
                S.dma("pool", MT[s, h_glob * 64:(h_glob + 1) * 64, qc * 512:(qc + 1) * 512], mg[k][0:64, :], reads=[bmg[k]])
            return fa, fb

        def alloc_work():
            sq = [A.bf16(512) for _ in range(2)]
            lnv = [A.f32(512) for _ in range(2)]
            rinv = [A.f32(512) for _ in range(2)]
            mg = [A.bf16(512) for _ in range(2)]
            return (sq, lnv, rinv, mg, [Buf(), Buf()], [Buf(), Buf()], [Buf(), Buf()], [Buf(), Buf()])

        A.reset()
        Vf = [A.bf16(32 * VROW) for _ in range(2)]
        bVf = [Buf(), Buf()]
        QA = [A.bf16(S_LEN) for _ in range(2)]
        KA = [A.bf16(S_LEN) for _ in range(2)]
        bQA = [Buf(), Buf()]
        bKA = [Buf(), Buf()]
        PT = [A.bf16(512) for _ in range(4)]
        bPT = [Buf() for _ in range(4)]
        work = alloc_work()
        for k in range(2):
            S.op("dve", lambda e, o=KA[k][64:67, :]: e.memset(o, 1.0), writes=[bKA[k]])
        blocks = []
        hcount = 0
        ocount = 0
        for s in range(LIM["A_seq"]):
            for h in range(LIM["A_heads"]):
                for qc in range(8):
                    nk = 4 * qc + 4
                    for kc in range(nk):
                        blocks.append((s, h, qc, kc, nk, hcount % 2, 3 + (ocount % 2)))
                    ocount += 1
                hcount += 1
        loaded_s = set()
        loaded_h = set()
        pending = deque()
        wi = 0

        def fox_head(i):
            s, h, qc, kc, nk, hb, ob = blocks[i]
            if s not in loaded_s:
                loaded_s.add(s)
                vsrc = VF[s].rearrange("(n p) e -> p n e", p=128)
                Vf3 = Vf[s % 2].rearrange("p (n e) -> p n e", e=VROW)
                for q4 in range(4):
                    S.dma("sp", Vf3[:, q4 * 8:(q4 + 1) * 8, :], vsrc[:, q4 * 8:(q4 + 1) * 8, :], writes=[bVf[s % 2]])
            if (s, h) not in loaded_h:
                loaded_h.add((s, h))
                S.dma("sp", QA[hb][0:64, :], QTF[s, h * 64:(h + 1) * 64, :], writes=[bQA[hb]])
                S.dma("sp", QA[hb][64:67, :], FR[s].rearrange("(j h) t -> h j t", h=8)[h], writes=[bQA[hb]])
                S.dma("sp", KA[hb][0:64, :], KTF[s, h * 64:(h + 1) * 64, :], writes=[bKA[hb]])
            j = kc - 4 * qc
            c0 = 128 * j if j > 0 else 0
            ncol = 512 - c0
            sb = (0, 1, 2, 7)[i % 4]
            STb = bank(sb)
            S.op("pe", lambda e, o=STb[:, 0:ncol], l=KA[hb][0:67, kc * 128:(kc + 1) * 128], r=QA[hb][0:67, qc * 512 + c0:(qc + 1) * 512], j=j:
                 e.matmul(o, lhsT=l, rhs=r, start=True, stop=(j < 0)), reads=[bQA[hb], bKA[hb]], writes=[bPS[sb]], inc=(j < 0))
            if j >= 0:
                S.op("pe", lambda e, o=STb[:, 0:128]: e.matmul(o, lhsT=identb, rhs=maskneg, start=False, stop=True),
                     reads=[bConst], writes=[bPS[sb]])

        def fox_tail(i):
            nonlocal wi
            s, h, qc, kc, nk, hb, ob = blocks[i]
            j = kc - 4 * qc
            c0 = 128 * j if j > 0 else 0
            ncol = 512 - c0
            sb = (0, 1, 2, 7)[i % 4]
            STb = bank(sb)
            pi = i % 4
            OT = bank(ob)
            nFv = negF[s].rearrange("p (n h) -> p n h", h=8)
            Vfv = Vf[s % 2].rearrange("p (n h e) -> p n h e", h=8, e=VW)
            S.op("act", lambda e, o=PT[pi][:, 0:ncol], i_=STb[:, 0:ncol], bi=nFv[:, kc, h:h + 1]:
                 e.activation(out=o, in_=i_, func=AF.Exp, bias=bi, scale=1.0), reads=[bPS[sb], bNegF[s]], writes=[bPT[pi]])
            S.op("pe", lambda e, o=OT[0:65, c0:512], l=Vfv[:, kc, h, 0:65], r=PT[pi][:, 0:ncol], kc=kc, nk=nk:
                 e.matmul(o, lhsT=l, rhs=r, start=(kc == 0), stop=(kc == nk - 1)), reads=[bVf[s % 2], bPT[pi]], writes=[bPS[ob]], inc=(kc == nk - 1))
            if kc == nk - 1:
                fa, fb = finalize_ops(OT, bPS[ob], h, s, qc, work, wi, (5, 6))
                wi += 1
                pending.append(fa)
                pending.append(fb)
            elif pending:
                pending.popleft()()

        LA = 3
        nb = len(blocks)
        for i in range(nb + LA):
            if i < nb:
                fox_head(i)
            if i >= LA:
                fox_tail(i - LA)
        while pending:
            pending.popleft()()
        S.barrier()

        if stop == 'fox':
            S.emit()
            return nc
        A.reset()
        V1 = A.bf16(32 * VROW)
        V4 = A.bf16(32 * VROW)
        V16 = A.bf16(32 * VROW)
        bVd = {1: Buf(), 4: Buf(), 16: Buf()}
        QDs = [A.bf16(S_LEN) for _ in range(2)]
        KDs = [A.bf16(S_LEN) for _ in range(2)]
        bQD = [Buf(), Buf()]
        bKD = [Buf(), Buf()]
        acc = [A.f32(S_LEN) for _ in range(2)]
        bacc = [Buf(), Buf()]
        PT = [A.bf16(512) for _ in range(4)]
        bPT = [Buf() for _ in range(4)]
        work = alloc_work()
        Vv = {1: V1.rearrange("p (n h e) -> p n h e", h=8, e=VW),
              4: V4.rearrange("p (n h e) -> p n h e", h=8, e=VW),
              16: V16.rearrange("p (n h e) -> p n h e", h=8, e=VW)}
        V3 = {1: V1.rearrange("p (n e) -> p n e", e=VROW), 4: V4.rearrange("p (n e) -> p n e", e=VROW),
              16: V16.rearrange("p (n e) -> p n e", e=VROW)}

        def blk(d, r, n):
            return r * (32 // d) + n

        def cols(d, r, n):
            st0 = d * 128 * n + r
            return slice(st0, st0 + d * 127 + 1, d)

        groups = []
        hcount = 0
        for s in range(LIM["A_seq"]):
            for h in range(LIM["A_heads"]):
                hb = hcount % 2
                hcount += 1
                ac = acc[hb]
                gl = []
                for g in range(8):
                    gl.append((1, [(0, 4 * g + jj) for jj in range(4)], ac[0:65, 512 * g:512 * (g + 1)].rearrange("p (j i) -> p j i", j=4)))
                for n in range(8):
                    gl.append((4, [(r, n) for r in range(4)], ac[0:65, 512 * n:512 * (n + 1)].rearrange("p (i r) -> p r i", r=4)))
                for n in range(2):
                    for r0 in range(0, 16, 4):
                        gl.append((16, [(r0 + jj, n) for jj in range(4)],
                                   ac[0:65, 2048 * n:2048 * (n + 1)].rearrange("p (i r) -> p r i", r=16)[:, r0:r0 + 4, :]))
                for gi_, (d, blocks_, accv) in enumerate(gl):
                    groups.append((s, h, hb, d, blocks_, accv, gi_ == len(gl) - 1))
        loaded_s = set()
        loaded_h = set()
        pending = deque()
        ginfo = {}

        def dil_head(gidx):
            s, h, hb, d, blocks_, accv, last = groups[gidx]
            if (s, h) not in loaded_h:
                loaded_h.add((s, h))
                S.dma("sp", QDs[hb][0:64, :], QTD[s, h * 64:(h + 1) * 64, :], writes=[bQD[hb]])
                S.dma("sp", KDs[hb][0:64, :], KTD[s, h * 64:(h + 1) * 64, :], writes=[bKD[hb]])
            if s not in loaded_s:
                loaded_s.add(s)
                vsrc = VD[s].rearrange("(n p) e -> p n e", p=128)
                for q4 in range(4):
                    S.dma("sp", V3[1][:, q4 * 8:(q4 + 1) * 8, :], vsrc[:, q4 * 8:(q4 + 1) * 8, :], writes=[bVd[1]])
                v4src = VD[s].rearrange("(n i r) e -> r i n e", n=8, i=128, r=4)
                for r in range(4):
                    S.dma("sp", V3[4][:, r * 8:(r + 1) * 8, :], v4src[r], writes=[bVd[4]])
                v16src = VD[s].rearrange("(n i r) e -> r i n e", n=2, i=128, r=16)
                for r in range(16):
                    S.dma("sp", V3[16][:, r * 2:(r + 1) * 2, :], v16src[r], writes=[bVd[16]])
            gp = gidx % 2
            info = []
            for half in range(2):
                sb = gp * 2 + half
                STv = bank(sb).rearrange("p (j h q) -> p j h q", j=2, h=2)
                pi = (gidx * 2 + half) % 4
                PTv = PT[pi].rearrange("p (j h q) -> p j h q", j=2, h=2)
                hasprev = []
                for jj in range(2):
                    r, n = blocks_[half * 2 + jj]
                    qcols = cols(d, r, n)
                    S.op("pe", lambda e, o=STv[:, jj, 1, :], l=KDs[hb][0:64, qcols], r_=QDs[hb][0:64, qcols]:
                         e.matmul(o, lhsT=l, rhs=r_, start=True, stop=True), reads=[bQD[hb], bKD[hb]], writes=[bPS[sb]],
                         inc=(jj == 1 and n < 1))
                    if n >= 1:
                        S.op("pe", lambda e, o=STv[:, jj, 0, :], l=KDs[hb][0:64, cols(d, r, n - 1)], r_=QDs[hb][0:64, qcols]:
                             e.matmul(o, lhsT=l, rhs=r_, start=True, stop=True), reads=[bQD[hb], bKD[hb]], writes=[bPS[sb]],
                             inc=(jj == 1))
                    hasprev.append(n >= 1)
                if all(hasprev):
                    S.op("act", lambda e, o=PT[pi], i_=bank(sb): e.activation(out=o, in_=i_, func=AF.Exp), reads=[bPS[sb]], writes=[bPT[pi]])
                    S.op("pool" if half == 0 else "dve", lambda e, o=PT[pi]: e.tensor_tensor(out=o, in0=o, in1=maskD2, op=ALU.mult), reads=[bPT[pi], bConst], writes=[bPT[pi]])
                else:
                    for jj in range(2):
                        lo = 0 if hasprev[jj] else 1
                        S.op("act", lambda e, o=PTv[:, jj, lo:2, :], i_=STv[:, jj, lo:2, :]: e.activation(out=o, in_=i_, func=AF.Exp),
                             reads=[bPS[sb]], writes=[bPT[pi]])
                        S.op("pool" if half == 0 else "dve", lambda e, o=PTv[:, jj, lo:2, :], m=md[:, jj, lo:2, :]: e.tensor_tensor(out=o, in0=o, in1=m, op=ALU.mult),
                             reads=[bPT[pi], bConst], writes=[bPT[pi]])
                info.append((pi, PTv, hasprev))
            ginfo[gidx] = info

        def dil_tail(gidx):
            nonlocal wi
            s, h, hb, d, blocks_, accv, last = groups[gidx]
            gp = gidx % 2
            ob = 6 + gp
            OT = bank(ob).rearrange("p (j i) -> p j i", j=4)
            info = ginfo.pop(gidx)
            for half in range(2):
                pi, PTv, hasprev = info[half]
                for jj in range(2):
                    r, n = blocks_[half * 2 + jj]
                    slot = half * 2 + jj
                    lastslot = (slot == 3)
                    S.op("pe", lambda e, o=OT[0:65, slot, :], l=Vv[d][:, blk(d, r, n), h, 0:65], r_=PTv[:, jj, 1, :], hp=hasprev[jj]:
                         e.matmul(o, lhsT=l, rhs=r_, start=True, stop=(not hp)), reads=[bVd[d], bPT[pi]], writes=[bPS[ob]],
                         inc=(lastslot and not hasprev[jj]))
                    if hasprev[jj]:
                        S.op("pe", lambda e, o=OT[0:65, slot, :], l=Vv[d][:, blk(d, r, n - 1), h, 0:65], r_=PTv[:, jj, 0, :]:
                             e.matmul(o, lhsT=l, rhs=r_, start=False, stop=True), reads=[bVd[d], bPT[pi]], writes=[bPS[ob]],
                             inc=lastslot)
            if d == 1:
                S.op("dve", lambda e, o=accv, i_=OT[0:65, :, :]: e.tensor_copy(out=o, in_=i_), reads=[bPS[ob]], writes=[bacc[hb]])
            else:
                S.op("dve", lambda e, o=accv, i_=OT[0:65, :, :]: e.tensor_tensor(out=o, in0=i_, in1=o, op=ALU.add),
                     reads=[bPS[ob], bacc[hb]], writes=[bacc[hb]])
            if last:
                ac = acc[hb]
                for qc in range(8):
                    fa, fb = finalize_ops(ac[:, qc * 512:(qc + 1) * 512], bacc[hb], NH + h, s, qc, work, wi, (4, 5))
                    wi += 1
                    pending.append(fa)
                    pending.append(fb)
            elif pending:
                pending.popleft()()

        ng = len(groups)
        done_tail = -1
        for g in range(ng + 1):
            if g < ng:
                if g >= 1 and groups[g][0] != groups[g - 1][0]:
                    dil_tail(g - 1)
                    done_tail = g - 1
                dil_head(g)
            if g >= 1 and done_tail != g - 1:
                dil_tail(g - 1)
        while pending:
            pending.popleft()()
        S.barrier()

        if stop == 'dil':
            S.emit()
            return nc
        def layer_norm(y, by, dst, bdst, gtile, btile, lnw, epsb, bLNc):
            stt, mv, rstd, nb, bst = lnw
            S.op("dve", lambda e: e.bn_stats(out=stt[:, 0:6], in_=y[:, 0:512]), reads=[by], writes=[bst])
            S.op("dve", lambda e: e.bn_stats(out=stt[:, 6:12], in_=y[:, 512:1024]), reads=[by], writes=[bst])
            S.op("dve", lambda e: e.bn_aggr(out=mv[:, 0:2], in_=stt[:, 0:12]), reads=[bst], writes=[bst])
            S.op("act", lambda e: e.activation(out=rstd, in_=mv[:, 1:2], func=AF.Sqrt, bias=epsb[:, 0:1], scale=1.0), reads=[bst, bConst], writes=[bst])
            S.op("dve", lambda e: e.reciprocal(out=rstd, in_=rstd), reads=[bst], writes=[bst])
            S.op("dve", lambda e: e.scalar_tensor_tensor(out=nb, in0=mv[:, 0:1], scalar=-1.0, in1=rstd, op0=ALU.mult, op1=ALU.mult), reads=[bst], writes=[bst])
            S.op("act", lambda e: e.activation(out=y, in_=y, func=AF.Identity, bias=nb[:, 0:1], scale=rstd[:, 0:1]), reads=[by, bst], writes=[by])
            S.op("pool", lambda e: e.tensor_tensor(out=y, in0=y, in1=gtile, op=ALU.mult), reads=[by, bLNc], writes=[by])
            S.op("pool", lambda e: e.tensor_tensor(out=dst, in0=y, in1=btile, op=ALU.add), reads=[by, bLNc], writes=[bdst])

        def alloc_lnw(n):
            return [(A.f32(12), A.f32(2), A.f32(1), A.f32(1), Buf()) for _ in range(n)]

        A.reset()
        A.limit = WUP_OFF
        wup = A.at_bf16(WUP_OFF, 8 * 2 * D_FF)
        wupv = wup.rearrange("p (c f) -> p c f", c=8)
        bWu = [Buf(), Buf()]
        wu_dv = wup_d.rearrange("(c p) f -> p c f", p=128)

        wup_jobs = []
        for halfsel in (0, 1):
            for c in range(8):
                for q4 in ((0, 2) if halfsel == 0 else (1, 3)):
                    wup_jobs.append((c, q4, halfsel))

        def load_wup_next():
            if wup_jobs:
                c, q4, halfsel = wup_jobs.pop(0)
                S.dma("pool", wupv[:, c, q4 * 1408:(q4 + 1) * 1408], wu_dv[:, c, q4 * 1408:(q4 + 1) * 1408], writes=[bWu[halfsel]])
        wout = A.bf16(8 * D)
        woutv = wout.rearrange("p (c f) -> p c f", c=8)
        bWo = Buf()
        wo_dv = wout_d.rearrange("(c p) f -> p c f", p=128)
        for c in range(8):
            S.dma("pool", woutv[:, c, :], wo_dv[:, c, :], writes=[bWo])
        bLNc1 = Buf()
        g1t = A.f32(1024)
        b1t = A.f32(1024)
        epsb1 = A.f32(1)
        S.op("dve", lambda e, t=epsb1: e.memset(t, LN_EPS), writes=[bConst])
        S.dma("sp", g1t, ln1g_d.broadcast_to([128, 1024]), writes=[bLNc1])
        S.dma("sp", b1t, ln1b_d.broadcast_to([128, 1024]), writes=[bLNc1])
        mtt = [A.bf16(8 * 512) for _ in range(2)]
        bmtt = [Buf(), Buf()]
        xt = [A.f32(1024) for _ in range(2)] * 2
        bxt = [Buf() for _ in range(2)] * 2
        tmpb = [A.f32(1024)] * 2
        btmp = [Buf()] * 2
        yb = [A.f32(1024) for _ in range(2)] * 2
        byb = [Buf() for _ in range(2)] * 2
        x1b = [A.f32(1024) for _ in range(3)]
        bx1 = [Buf() for _ in range(3)]
        lnw = alloc_lnw(2)
        h2st = [A.bf16(8 * 512) for _ in range(2)]
        bh2 = [Buf(), Buf()]
        zer = A.bf16(16)
        S.op("dve", lambda e, t=zer: e.memset(t, 0.0), writes=[bConst])
        tiles = [(s, J, i) for s in range(LIM["B_seq"]) for J in range(8) for i in range(4)]
        for s in range(LIM["B_seq"]):
            for c in range(8):
                S.dma("pool", H2T[s, c * 128:(c + 1) * 128, 0:2], zer[:, 0:2], reads=[bConst])

        def b1_load_mt(s, J):
            jb = (s * 8 + J) % 2
            mt_v = MT[s].rearrange("(c p) t -> p c t", p=128)
            mv3 = mtt[jb].rearrange("p (c t) -> p c t", c=8)
            for c in range(8):
                S.dma("sp", mv3[:, c, :], mt_v[:, c, J * 512:(J + 1) * 512], writes=[bmtt[jb]])

        def b1_head(t):
            s, J, i = tiles[t]
            if t >= 2 and (t % 2 == 0 or len(tiles) - t <= len(wup_jobs)):
                load_wup_next()
            jb = (s * 8 + J) % 2
            if i == 0:
                if t == 0:
                    b1_load_mt(s, J)
                if t + 4 < len(tiles):
                    b1_load_mt(tiles[t + 4][0], tiles[t + 4][1])
            mv3 = mtt[jb].rearrange("p (c t) -> p c t", c=8)
            T = J * 4 + i
            tok = slice(i * 128, (i + 1) * 128)
            S.dma("sp", xt[t % 2], x_d[s, T * 128:(T + 1) * 128, :], writes=[bxt[t % 2]])
            mixb = (2 * (t % 3), 2 * (t % 3) + 1)
            for half in range(2):
                for c in range(8):
                    S.op("pe", lambda e, o=bank(mixb[half]), l=mv3[:, c, tok], r=woutv[:, c, half * 512:(half + 1) * 512], c=c:
                         e.matmul(o, lhsT=l, rhs=r, start=(c == 0), stop=(c == 7)), reads=[bmtt[jb], bWo], writes=[bPS[mixb[half]]], inc=(c == 7))

        def b1_tail(t):
            s, J, i = tiles[t]
            jb = (s * 8 + J) % 2
            T = J * 4 + i
            tok = slice(i * 128, (i + 1) * 128)
            k3 = t % 3
            mixb = (2 * (t % 3), 2 * (t % 3) + 1)
            mix = PS[:, mixb[0] * 512:(mixb[0] + 2) * 512]
            tmp = tmpb[t % 2]
            S.op("dve", lambda e, i0=mix, g=gab[s], tmp=tmp: e.tensor_tensor(out=tmp, in0=i0, in1=g, op=ALU.mult),
                 reads=[bPS[mixb[0]], bPS[mixb[1]], bG], writes=[btmp[t % 2]])
            S.op("dve", lambda e, o=yb[t % 2], xx=xt[t % 2], tmp=tmp: e.scalar_tensor_tensor(out=o, in0=xx, scalar=ALPHA, in1=tmp, op0=ALU.mult, op1=ALU.add),
                 reads=[bxt[t % 2], btmp[t % 2]], writes=[byb[t % 2]])
            layer_norm(yb[t % 2], byb[t % 2], x1b[k3], bx1[k3], g1t, b1t, lnw[t % 2], epsb1, bLNc1)
            S.dma("pool", X1[s, T * 128:(T + 1) * 128, :], x1b[k3], reads=[bx1[k3]])

        def b1_tail2(t):
            s, J, i = tiles[t]
            jb = (s * 8 + J) % 2
            tok = slice(i * 128, (i + 1) * 128)
            k3 = t % 3
            h2v = h2st[jb].rearrange("p (c t) -> p c t", c=8)
            for c in range(8):
                pbk = 6 + c // 4
                S.op("pe", lambda e, o=bank(pbk)[:, (c % 4) * 128:(c % 4 + 1) * 128], i_=x1b[k3][:, c * 128:(c + 1) * 128]:
                     e.transpose(o, i_, ident32), reads=[bx1[k3], bConst], writes=[bPS[pbk]])
            for c in range(8):
                pbk = 6 + c // 4
                S.op("act", lambda e, o=h2v[:, c, tok], i_=bank(pbk)[:, (c % 4) * 128:(c % 4 + 1) * 128],
                     sca=sc1f[:, s * 8 + c:s * 8 + c + 1], bi=shf[:, s * 8 + c:s * 8 + c + 1]:
                     e.activation(out=o, in_=i_, func=AF.Identity, bias=bi, scale=sca), reads=[bPS[pbk], bAda], writes=[bh2[jb]])
            if i == 3:
                h2_v = H2T[s].rearrange("(c p) t -> p c t", p=128)
                for c in range(8):
                    S.dma("pool", h2_v[:, c, 2 + J * 512:2 + (J + 1) * 512], h2v[:, c, :], reads=[bh2[jb]])

        nt_ = len(tiles)
        for t in range(nt_ + 2):
            if t < nt_:
                b1_head(t)
            if 1 <= t <= nt_:
                b1_tail(t - 1)
            if t >= 2:
                b1_tail2(t - 2)
        S.barrier()

        if stop == 'B1':
            S.emit()
            return nc
        A.reset()
        A.limit = WDN_OFF
        while wup_jobs:
            load_wup_next()
        wdn = A.at_bf16(WDN_OFF, NJ * D)
        wdnv = wdn.rearrange("p (j f) -> p j f", j=NJ)
        bWd = Buf()
        wd_dv = wdn_d.rearrange("(j p) f -> p j f", p=128)
        wdn_jobs = list(range(NJ))

        def load_wdn_next():
            if wdn_jobs:
                j = wdn_jobs.pop(0)
                S.dma("pool", wdnv[:, j, :], wd_dv[:, j, :], writes=[bWd])
        h2t = [A.bf16(8 * 512) for _ in range(2)]
        bh2t = [Buf(), Buf()]
        ya = [A.f32(512) for _ in range(2)]
        yg = [A.f32(512) for _ in range(2)]
        sg = [A.f32(512) for _ in range(2)]
        actst = [A.bf16(512) for _ in range(3)]
        bya = [Buf(), Buf()]
        byg = [Buf(), Buf()]
        bsg = [Buf(), Buf()]
        bact = [Buf() for _ in range(3)]
        cwv = cw.rearrange("p (i j) -> p i j", i=3)
        ftiles = [(s, J) for s in range(LIM["B_seq"]) for J in range(9)]
        items = [(ti, j) for ti in range(len(ftiles)) for j in range(NJ)]

        def b2a_load(ti):
            s, J = ftiles[ti]
            t0 = 510 * J
            N = min(510, S_LEN - t0) + 2
            h2_v = H2T[s].rearrange("(c p) t -> p c t", p=128)
            hv = h2t[ti % 2].rearrange("p (c t) -> p c t", c=8)
            for c in range(8):
                S.dma("sp", hv[:, c, 0:N], h2_v[:, c, t0:t0 + N], writes=[bh2t[ti % 2]])

        def b2a_head(i):
            ti, j = items[i]
            if i % 12 == 6 or len(items) - i <= len(wdn_jobs):
                load_wdn_next()
            s, J = ftiles[ti]
            if j == 0:
                if ti == 0:
                    b2a_load(0)
                if ti + 1 < len(ftiles):
                    b2a_load(ti + 1)
            N = min(510, S_LEN - 510 * J) + 2
            hv = h2t[ti % 2].rearrange("p (c t) -> p c t", c=8)
            ub = (2 * (i % 4), 2 * (i % 4) + 1)
            for which in range(2):
                col0 = which * D_FF + j * 128
                for c in range(8):
                    S.op("pe", lambda e, o=bank(ub[which])[:, 0:N], l=wupv[:, c, col0:col0 + 128], r=hv[:, c, 0:N], c=c:
                         e.matmul(o, lhsT=l, rhs=r, start=(c == 0), stop=(c == 7)), reads=[bWu[0 if j < 11 else 1], bh2t[ti % 2]], writes=[bPS[ub[which]]], inc=(c == 7))

        def b2a_tail(i):
            ti, j = items[i]
            s, J = ftiles[ti]
            t0 = 510 * J
            ntok = min(510, S_LEN - t0)
            N = ntok + 2
            k = i % 2
            k3 = i % 3
            ub = (2 * (i % 4), 2 * (i % 4) + 1)
            for which, ydst, by_ in ((0, ya[k], bya[k]), (1, yg[k], byg[k])):
                u = bank(ub[which])
                jj = which * NJ + j
                S.op("act", lambda e, o=ydst[:, 0:ntok], i_=u[:, 2:N], sca=cwv[:, 2, jj:jj + 1], bi=cb[:, jj:jj + 1]:
                     e.activation(out=o, in_=i_, func=AF.Identity, bias=bi, scale=sca), reads=[bPS[ub[which]], bConst], writes=[by_])
                S.op("dve", lambda e, o=ydst[:, 0:ntok], i_=u[:, 1:N - 1], sca=cwv[:, 1, jj:jj + 1]:
                     e.scalar_tensor_tensor(out=o, in0=i_, scalar=sca, in1=o, op0=ALU.mult, op1=ALU.add), reads=[bPS[ub[which]], by_, bConst], writes=[by_])
                S.op("dve", lambda e, o=ydst[:, 0:ntok], i_=u[:, 0:N - 2], sca=cwv[:, 0, jj:jj + 1]:
                     e.scalar_tensor_tensor(out=o, in0=i_, scalar=sca, in1=o, op0=ALU.mult, op1=ALU.add), reads=[bPS[ub[which]], by_, bConst], writes=[by_])
            S.op("act", lambda e, o=sg[k][:, 0:ntok], i_=yg[k][:, 0:ntok]: e.activation(out=o, in_=i_, func=AF.Silu), reads=[byg[k]], writes=[bsg[k]])
            S.op("pool", lambda e, o=actst[k3][:, 0:ntok], a=sg[k][:, 0:ntok], b=ya[k][:, 0:ntok]: e.tensor_tensor(out=o, in0=a, in1=b, op=ALU.mult),
                 reads=[bsg[k], bya[k]], writes=[bact[k3]])
            S.dma("pool", ACTT[s, j * 128:(j + 1) * 128, t0:t0 + ntok], actst[k3][:, 0:ntok], reads=[bact[k3]])

        LA = 2
        for i in range(len(items) + LA):
            if i < len(items):
                b2a_head(i)
            if i >= LA:
                b2a_tail(i - LA)
        S.barrier()

        if stop == 'B2a':
            S.emit()
            return nc
        while wdn_jobs:
            load_wdn_next()
        A.reset()
        A.limit = WDN_OFF
        bLNc2 = Buf()
        g2t = A.f32(1024)
        b2t = A.f32(1024)
        S.dma("sp", g2t, ln2g_d.broadcast_to([128, 1024]), writes=[bLNc2])
        S.dma("sp", b2t, ln2b_d.broadcast_to([128, 1024]), writes=[bLNc2])
        epsb2 = A.f32(1)
        S.op("dve", lambda e, t=epsb2: e.memset(t, LN_EPS), writes=[bConst])
        low_top = A.top
        A.top = WUP_OFF
        A.limit = NW
        att = [A.bf16(NJ * 512) for _ in range(2)]
        batt = [Buf(), Buf()]
        x1t = [A.f32(1024) for _ in range(3)]
        bx1t = [Buf() for _ in range(3)]
        yb = [A.f32(1024) for _ in range(3)]
        byb = [Buf() for _ in range(3)]
        ob_ = [A.f32(1024) for _ in range(3)]
        bob = [Buf() for _ in range(3)]
        A.top = low_top
        A.limit = WDN_OFF
        tmpb = [A.f32(1024) for _ in range(2)]
        btmp = [Buf(), Buf()]
        lnw = alloc_lnw(2)
        tiles = [(s, J, i) for s in range(LIM["B_seq"]) for J in range(8) for i in range(4)]

        def b2b_load(s, J):
            jb = (s * 8 + J) % 2
            at_v = ACTT[s].rearrange("(j p) t -> p j t", p=128)
            av = att[jb].rearrange("p (j t) -> p j t", j=NJ)
            for j in range(NJ):
                S.dma("sp", av[:, j, :], at_v[:, j, J * 512:(J + 1) * 512], writes=[batt[jb]])

        def b2b_head(t):
            s, J, i = tiles[t]
            jb = (s * 8 + J) % 2
            if i == 0:
                if t == 0:
                    b2b_load(s, J)
                if t + 4 < len(tiles):
                    b2b_load(tiles[t + 4][0], tiles[t + 4][1])
            av = att[jb].rearrange("p (j t) -> p j t", j=NJ)
            T = J * 4 + i
            tok = slice(i * 128, (i + 1) * 128)
            S.dma("sp", x1t[t % 3], X1[s, T * 128:(T + 1) * 128, :], writes=[bx1t[t % 3]])
            fb = (2 * (t % 4), 2 * (t % 4) + 1)
            for half in range(2):
                for j in range(NJ):
                    S.op("pe", lambda e, o=bank(fb[half]), l=av[:, j, tok], r=wdnv[:, j, half * 512:(half + 1) * 512], j=j:
                         e.matmul(o, lhsT=l, rhs=r, start=(j == 0), stop=(j == NJ - 1)), reads=[batt[jb], bWd], writes=[bPS[fb[half]]], inc=(j == NJ - 1))

        def b2b_tail(t):
            s, J, i = tiles[t]
            T = J * 4 + i
            k3 = t % 3
            fb = (2 * (t % 4), 2 * (t % 4) + 1)
            ffn = PS[:, fb[0] * 512:(fb[0] + 2) * 512]
            tmp = tmpb[t % 2]
            S.op("dve", lambda e, i0=ffn, g=gfb[s], tmp=tmp: e.tensor_tensor(out=tmp, in0=i0, in1=g, op=ALU.mult),
                 reads=[bPS[fb[0]], bPS[fb[1]], bG], writes=[btmp[t % 2]])
            S.op("dve", lambda e, o=yb[k3], xx=x1t[k3], tmp=tmp: e.scalar_tensor_tensor(out=o, in0=xx, scalar=ALPHA, in1=tmp, op0=ALU.mult, op1=ALU.add),
                 reads=[bx1t[k3], btmp[t % 2]], writes=[byb[k3]])
            layer_norm(yb[k3], byb[k3], ob_[k3], bob[k3], g2t, b2t, lnw[t % 2], epsb2, bLNc2)
            S.dma("pool", out_d[s, T * 128:(T + 1) * 128, :], ob_[k3], reads=[bob[k3]])

        LA = 2
        for t in range(len(tiles) + LA):
            if t < len(tiles):
                b2b_head(t)
            if t >= LA:
                b2b_tail(t - LA)
        S.barrier()
        S.emit()
    return nc


_PROG = None


def _prep_inputs(core, x, c, positions, w_ada, b_ada, w_in, b_fgate, gn_a, gn_b, w_out,
                 ln1_g, ln1_b, w_up, conv_w, conv_b, w_down, ln2_g, ln2_b, consts):
    b0 = core * NSEQ
    xs = np.ascontiguousarray(x[b0:b0 + NSEQ])
    m = {
        "x": xs,
        "xT": np.ascontiguousarray(xs.transpose(0, 2, 1)),
        "c_arr": np.ascontiguousarray(c[b0:b0 + NSEQ].T.reshape(8, 128, NSEQ).transpose(1, 0, 2).reshape(128, 16)),
        "pos_arr": np.ascontiguousarray(positions[b0:b0 + NSEQ].reshape(NSEQ, 32, 128).transpose(0, 2, 1)),
    }
    m.update(consts)
    return m


def kernel(x, c, positions, w_ada, b_ada, w_in, b_fgate, gn_a, gn_b, w_out,
           ln1_g, ln1_b, w_up, conv_w, conv_b, w_down, ln2_g, ln2_b):
    global _PROG
    f32 = np.float32
    x = np.asarray(x, f32)
    c = np.asarray(c, f32)
    positions = np.asarray(positions, np.int32)
    a = lambda t: np.ascontiguousarray(np.asarray(t, f32))
    kq = np.arange(128)
    tri = (kq[None, :] >= kq[:, None]).astype(f32)
    freq = (500000.0 ** (-np.arange(0, 16, 2, dtype=np.float32) / np.float32(16))).astype(f32)
    consts = {
        "w_ada": a(w_ada[0]),
        "b_ada_fm": a(np.asarray(b_ada[0], f32).reshape(48, 128).T),
        "b_ada_row": a(np.asarray(b_ada[0], f32).reshape(1, 6 * D)),
        "w_in": a(w_in[0]),
        "b_fgate": a(np.asarray(b_fgate[0], f32).reshape(1, 8)),
        "gn": a(np.concatenate([np.asarray(gn_a[0], f32).reshape(8, 64).T, np.asarray(gn_b[0], f32).reshape(8, 64).T], axis=1)),
        "w_out": a(w_out[0]),
        "ln1_g": a(np.asarray(ln1_g[0], f32).reshape(1, D)),
        "ln1_b": a(np.asarray(ln1_b[0], f32).reshape(1, D)),
        "w_up": a(w_up[0]),
        "conv_wf": a(np.asarray(conv_w[0], f32).reshape(3, 44, 128).transpose(2, 0, 1).reshape(128, 132)),
        "conv_bf": a(np.asarray(conv_b[0], f32).reshape(44, 128).T),
        "w_down": a(w_down[0]),
        "ln2_g": a(np.asarray(ln2_g[0], f32).reshape(1, D)),
        "ln2_b": a(np.asarray(ln2_b[0], f32).reshape(1, D)),
        "ident": np.eye(128, dtype=f32),
        "tri": tri,
        "triT": np.ascontiguousarray(tri.T),
        "freq": np.ascontiguousarray(np.broadcast_to(freq[None, :], (128, 8))),
    }
    if _PROG is None:
        _PROG = build_program()
    in_maps = [_prep_inputs(core, x, c, positions, None, None, None, None, None, None, None,
                            None, None, None, None, None, None, None, None, consts) for core in range(8)]
    res = run_bass_kernel_spmd(_PROG, in_maps, core_ids=list(range(8)))
    out = np.concatenate([np.asarray(r["out"], f32) for r in res.results], axis=0)
    kernel.last_results = res.results
    return out
```
